# Optimizing a Trainium2 kernel written in Bass

```python
import math
import jax, jax.numpy as jnp
from jax import lax
import numpy as np

D_MODEL = 2048
BATCH = 4
SEQ = 2048
DEPTH = 1

CHUNK = 64
Q_BLOCK = 128

N_HEADS = 16
QK_NOPE = 128
QK_ROPE = 64
V_HEAD = 128
Q_LORA = 512
KV_LORA = 512
ROPE_THETA = 10000.0
ATTN_SCALE = (QK_NOPE + QK_ROPE) ** -0.5

CONV_WIDTH = D_MODEL
CONV_K = 3

D_FF = ((8 * D_MODEL + 3 * 256 - 1) // (3 * 256)) * 256

COL_Q_A = Q_LORA
COL_KV_A = KV_LORA
COL_K_ROPE = QK_ROPE
COL_CONV_B = CONV_WIDTH
COL_CONV_C = CONV_WIDTH
COL_CONV_X = CONV_WIDTH
COL_GATE_A = D_MODEL
COL_GATE_B = D_MODEL
D_IN_ALL = COL_Q_A + COL_KV_A + COL_K_ROPE + COL_CONV_B + COL_CONV_C + COL_CONV_X + COL_GATE_A + COL_GATE_B
SPLIT_POINTS = (
    COL_Q_A,
    COL_Q_A + COL_KV_A,
    COL_Q_A + COL_KV_A + COL_K_ROPE,
    COL_Q_A + COL_KV_A + COL_K_ROPE + COL_CONV_B,
    COL_Q_A + COL_KV_A + COL_K_ROPE + COL_CONV_B + COL_CONV_C,
    COL_Q_A + COL_KV_A + COL_K_ROPE + COL_CONV_B + COL_CONV_C + COL_CONV_X,
    COL_Q_A + COL_KV_A + COL_K_ROPE + COL_CONV_B + COL_CONV_C + COL_CONV_X + COL_GATE_A,
)

DEEPNORM_ALPHA = (2.0 * DEPTH) ** 0.25
DEEPNORM_BETA = (8.0 * DEPTH) ** -0.25
LN_EPS = 1e-5
RMS_EPS = 1e-6

kernel_name = "hybrid_mla_shortconv_swiglu_deepnorm_adaln"


def layer_norm(x, g, b):
    xf = x.astype(jnp.float32)
    mu = jnp.mean(xf, axis=-1, keepdims=True)
    var = jnp.mean(jnp.square(xf - mu), axis=-1, keepdims=True)
    y = (xf - mu) * lax.rsqrt(var + LN_EPS)
    return (y * g.astype(jnp.float32) + b.astype(jnp.float32)).astype(x.dtype)


def rms_norm(x, g):
    xf = x.astype(jnp.float32)
    y = xf * lax.rsqrt(jnp.mean(jnp.square(xf), axis=-1, keepdims=True) + RMS_EPS)
    return (y * g.astype(jnp.float32)).astype(x.dtype)


def rope_tables(positions):
    inv_freq = 1.0 / (ROPE_THETA ** (jnp.arange(0, QK_ROPE, 2, dtype=jnp.float32) / QK_ROPE))
    ang = positions.astype(jnp.float32)[..., None] * inv_freq
    return jnp.cos(ang), jnp.sin(ang)


def apply_rope(x, cos, sin):
    xf = x.astype(jnp.float32)
    x1, x2 = jnp.split(xf, 2, axis=-1)
    out = jnp.concatenate([x1 * cos - x2 * sin, x2 * cos + x1 * sin], axis=-1)
    return out.astype(x.dtype)


def chunk_causal_mla_attention(q_nope, q_rope, k_nope, k_rope, v):
    S = q_nope.shape[1]
    outs = []
    for i in range(S // Q_BLOCK):
        q0 = i * Q_BLOCK
        L = q0 + Q_BLOCK
        s = (jnp.einsum('bqhd,bkhd->bhqk', q_nope[:, q0:L], k_nope[:, :L])
             + jnp.einsum('bqhr,bkr->bhqk', q_rope[:, q0:L], k_rope[:, :L]))
        s = s.astype(jnp.float32) * ATTN_SCALE
        q_chunk = (q0 + jnp.arange(Q_BLOCK)) // CHUNK
        k_chunk = jnp.arange(L) // CHUNK
        allowed = k_chunk[None, :] <= q_chunk[:, None]
        s = jnp.where(allowed[None, None], s, jnp.float32(-1e30))
        p = jax.nn.softmax(s, axis=-1).astype(v.dtype)
        outs.append(jnp.einsum('bhqk,bkhd->bqhd', p, v[:, :L]))
    return jnp.concatenate(outs, axis=1)


def causal_depthwise_conv(z, w_conv):
    S = z.shape[1]
    zp = jnp.pad(z, ((0, 0), (CONV_K - 1, 0), (0, 0)))
    out = zp[:, 0:S] * w_conv[0]
    for k in range(1, CONV_K):
        out = out + zp[:, k:k + S] * w_conv[k]
    return out


def setup_inputs(seed: int = 0) -> dict:
    key = jax.random.key(seed)
    ks = jax.random.split(key, 24)
    f32 = jnp.float32

    def nrm(k, shape, scale):
        return jax.random.normal(k, shape, f32) * scale

    x = nrm(ks[0], (BATCH, SEQ, D_MODEL), 1.0)
    c = nrm(ks[1], (BATCH, D_MODEL), 1.0)
    offsets = jax.random.randint(ks[2], (BATCH, 1), 0, 64, dtype=jnp.int32) * CHUNK
    positions = (offsets + jnp.arange(SEQ, dtype=jnp.int32)[None, :]).astype(jnp.int32)

    inputs = {
        "x": x,
        "c": c,
        "positions": positions,
        "w_ada": nrm(ks[3], (DEPTH, D_MODEL, 6 * D_MODEL), 0.5 * D_MODEL ** -0.5),
        "b_ada": nrm(ks[4], (DEPTH, 6 * D_MODEL), 0.01),
        "w_in": nrm(ks[5], (DEPTH, D_MODEL, D_IN_ALL), D_MODEL ** -0.5),
        "g_q_a": 1.0 + nrm(ks[6], (DEPTH, Q_LORA), 0.02),
        "w_q_b": nrm(ks[7], (DEPTH, Q_LORA, N_HEADS * (QK_NOPE + QK_ROPE)), Q_LORA ** -0.5),
        "g_kv_a": 1.0 + nrm(ks[8], (DEPTH, KV_LORA), 0.02),
        "w_kv_b": nrm(ks[9], (DEPTH, KV_LORA, N_HEADS * (QK_NOPE + V_HEAD)), KV_LORA ** -0.5),
        "w_o_a": nrm(ks[10], (DEPTH, N_HEADS * V_HEAD, D_MODEL), (N_HEADS * V_HEAD) ** -0.5 * DEEPNORM_BETA),
        "w_conv": nrm(ks[11], (DEPTH, CONV_K, CONV_WIDTH), CONV_K ** -0.5),
        "w_o_b": nrm(ks[12], (DEPTH, CONV_WIDTH, D_MODEL), CONV_WIDTH ** -0.5 * DEEPNORM_BETA),
        "w_o": nrm(ks[13], (DEPTH, D_MODEL, D_MODEL), D_MODEL ** -0.5 * DEEPNORM_BETA),
        "ln1_g": 1.0 + nrm(ks[14], (DEPTH, D_MODEL), 0.02),
        "ln1_b": nrm(ks[15], (DEPTH, D_MODEL), 0.02),
        "w_ffn_in": nrm(ks[16], (DEPTH, D_MODEL, 2 * D_FF), D_MODEL ** -0.5),
        "w_ffn_out": nrm(ks[17], (DEPTH, D_FF, D_MODEL), D_FF ** -0.5 * DEEPNORM_BETA),
        "ln2_g": 1.0 + nrm(ks[18], (DEPTH, D_MODEL), 0.02),
        "ln2_b": nrm(ks[19], (DEPTH, D_MODEL), 0.02),
    }
    return inputs


def reference(x, c, positions, w_ada, b_ada, w_in, g_q_a, w_q_b, g_kv_a, w_kv_b, w_o_a,
              w_conv, w_o_b, w_o, ln1_g, ln1_b, w_ffn_in, w_ffn_out, ln2_g, ln2_b):
    B, S, D = x.shape
    cos, sin = rope_tables(positions)
    cos_q, sin_q = cos[:, :, None, :], sin[:, :, None, :]
    c_act = jax.nn.silu(c)

    for l in range(DEPTH):
        mod = c_act @ w_ada[l] + b_ada[l]
        shift1, scale1, gate1, shift2, scale2, gate2 = [m[:, None, :] for m in jnp.split(mod, 6, axis=-1)]

        u = x * (1.0 + scale1) + shift1
        proj = u @ w_in[l]
        q_a, kv_a, k_rope, conv_b, conv_c, conv_x, gate_a, gate_b = jnp.split(proj, SPLIT_POINTS, axis=-1)

        q = (rms_norm(q_a, g_q_a[l]) @ w_q_b[l]).reshape(B, S, N_HEADS, QK_NOPE + QK_ROPE)
        q_nope, q_rope = q[..., :QK_NOPE], apply_rope(q[..., QK_NOPE:], cos_q, sin_q)
        kv = (rms_norm(kv_a, g_kv_a[l]) @ w_kv_b[l]).reshape(B, S, N_HEADS, QK_NOPE + V_HEAD)
        k_nope, v = kv[..., :QK_NOPE], kv[..., QK_NOPE:]
        k_rope = apply_rope(k_rope, cos, sin)
        attn = chunk_causal_mla_attention(q_nope, q_rope, k_nope, k_rope, v)
        y_a = attn.reshape(B, S, N_HEADS * V_HEAD) @ w_o_a[l]

        z = conv_c * conv_x
        y_b = (conv_b * causal_depthwise_conv(z, w_conv[l])) @ w_o_b[l]

        merged = jax.nn.sigmoid(gate_a) * y_a + jax.nn.sigmoid(gate_b) * y_b
        mix_out = merged @ w_o[l]
        x = layer_norm(DEEPNORM_ALPHA * x + gate1 * mix_out, ln1_g[l], ln1_b[l])

        u2 = x * (1.0 + scale2) + shift2
        h_gate, h_up = jnp.split(u2 @ w_ffn_in[l], 2, axis=-1)
        ffn_out = (jax.nn.silu(h_gate) * h_up) @ w_ffn_out[l]
        x = layer_norm(DEEPNORM_ALPHA * x + gate2 * ffn_out, ln2_g[l], ln2_b[l])

    return x
```

```python
import math
from contextlib import ExitStack

import numpy as np
import concourse.bass as bass
import concourse.mybir as mybir
from concourse.bass_utils import run_bass_kernel_spmd

F32 = mybir.dt.float32
BF16 = mybir.dt.bfloat16
I32 = mybir.dt.int32
AF = mybir.ActivationFunctionType
ALU = mybir.AluOpType

D = 2048
S = 2048
T = 1024
NH = 16
DFF = 5632
KC = D // 128
ALPHA = (2.0 * 1) ** 0.25
LN_EPS = 1e-5
RMS_EPS = 1e-6
ATTN_SCALE = (128 + 64) ** -0.5
NEG = -1.0e30

SELF_SYNC = True
ENGS = ["pe", "act", "dve", "pool", "sp"]


class _Op:
    __slots__ = ("eng", "fn", "r", "w", "pw", "dma", "dma_cnt", "barrier", "signal", "waits", "nodep")

    def __init__(self, eng, fn, r, w, dma=None, barrier=False, nodep=False, pw=()):
        self.eng = eng
        self.fn = fn
        self.r = tuple(r)
        self.w = tuple(w)
        self.pw = tuple(pw)
        self.dma = dma
        self.dma_cnt = 0
        self.barrier = barrier
        self.signal = False
        self.waits = []
        self.nodep = nodep


class Sched:
    def __init__(self):
        self.ops = []

    def op(self, eng, fn, r=(), w=(), pw=()):
        self.ops.append(_Op(eng, fn, r, w, pw=pw))

    def dma(self, q, out, in_, r=(), w=(), sem=None, nodep=False):
        assert sem is not None
        self.ops.append(_Op(q, lambda e, o=out, i=in_: e.dma_start(out=o, in_=i), r, w, dma=sem, nodep=nodep))

    def barrier(self):
        self.ops.append(_Op("sp", lambda e: e.nop(nofuse=True), (), (), barrier=True))

    def analyze(self):
        ops = self.ops
        last_w = {}
        readers = {}
        pwriters = {}
        eng_last = {}
        dma_count = {}
        fence = None
        known_e = {e: {e2: -1 for e2 in ENGS} for e in ENGS}
        known_d = {e: {} for e in ENGS}
        for i, op in enumerate(ops):
            deps = set()
            dwait = {}
            if op.barrier:
                deps.update(eng_last.values())
                for k, c in dma_count.items():
                    dwait[k] = c
            elif not op.nodep:
                for k in op.r:
                    j = last_w.get(k)
                    if j is not None:
                        deps.add(j)
                    pwk = pwriters.get(k)
                    if pwk:
                        deps.update(pwk.values())
                for k in op.w:
                    j = last_w.get(k)
                    if j is not None:
                        deps.add(j)
                    rd = readers.get(k)
                    if rd:
                        deps.update(rd.values())
                    pwk = pwriters.get(k)
                    if pwk:
                        deps.update(pwk.values())
                for k in op.pw:
                    j = last_w.get(k)
                    if j is not None:
                        deps.add(j)
                    rd = readers.get(k)
                    if rd:
                        deps.update(rd.values())
                if fence is not None:
                    deps.add(fence)
            deps.discard(i)
            emax = {}
            for j in deps:
                oj = ops[j]
                if oj.dma is not None:
                    if dwait.get(oj.dma, 0) < oj.dma_cnt:
                        dwait[oj.dma] = oj.dma_cnt
                else:
                    if oj.eng == op.eng and (op.eng == "pe" or not SELF_SYNC) and op.dma is None:
                        continue
                    if emax.get(oj.eng, -1) < j:
                        emax[oj.eng] = j
            x = op.eng
            for e2, j in emax.items():
                if known_e[x][e2] >= j:
                    continue
                known_e[x][e2] = j
                ops[j].signal = True
                op.waits.append(("e", e2, j))
            for k, c in dwait.items():
                if known_d[x].get(k, 0) >= c:
                    continue
                known_d[x][k] = c
                op.waits.append(("d", k, c))
            if op.dma is not None:
                dma_count[op.dma] = dma_count.get(op.dma, 0) + 1
                op.dma_cnt = dma_count[op.dma]
            else:
                eng_last[op.eng] = i
                known_e[x][x] = max(known_e[x][x], -1)
            if op.barrier:
                op.signal = True
                last_w.clear()
                readers.clear()
                pwriters.clear()
                fence = i
            else:
                for k in op.r:
                    rd = readers.setdefault(k, {})
                    if op.dma is not None:
                        rd[("dma", i)] = i
                    else:
                        rd[op.eng] = i
                for k in op.w:
                    last_w[k] = i
                    readers[k] = {}
                    pwriters[k] = {}
                for k in op.pw:
                    pwriters.setdefault(k, {})[op.eng if op.dma is None else ("dma", i)] = i
        cnt = {e: 0 for e in ENGS}
        self.sigval = {}
        for i, op in enumerate(ops):
            if op.dma is None and op.signal:
                cnt[op.eng] += 1
                self.sigval[i] = cnt[op.eng]
        self.dma_keys = sorted({op.dma for op in ops if op.dma is not None}, key=str)

    def emit(self, nc, es):
        self.analyze()
        esem = {e: es.enter_context(nc.semaphore("e_" + e)) for e in ENGS}
        dsem = {k: es.enter_context(nc.semaphore("d%d" % i)) for i, k in enumerate(self.dma_keys)}
        per = {e: [] for e in ENGS}
        for i, op in enumerate(self.ops):
            per[op.eng].append((i, op))
        sigval = self.sigval

        def run(name, eng):
            for i, op in per[name]:
                for kind, key, val in op.waits:
                    if kind == "e":
                        eng.wait_ge(esem[key], sigval[val])
                    else:
                        eng.wait_ge(dsem[key], 16 * val)
                ins = op.fn(eng)
                if op.dma is not None:
                    ins.then_inc(dsem[op.dma], 16)
                elif op.signal:
                    ins.then_inc(esem[name], 1)

        block = es.enter_context(nc.Block())
        block.tensor(lambda e: run("pe", e))
        block.scalar(lambda e: run("act", e))
        block.vector(lambda e: run("dve", e))
        block.gpsimd(lambda e: run("pool", e))
        block.sync(lambda e: run("sp", e))


STOP = None
NCORES = 8
DUMPS = ()


def build_program(stop=None, dumps=()):
    nc = bass.Bass("TRN2", target_bir_lowering=False)
    sc = Sched()
    es = ExitStack()

    def din(name, shape, dt=F32):
        return nc.dram_tensor(name, list(shape), dt, kind="ExternalInput").ap()

    x_all = din("x_all", [S, D])
    x_halo = din("x_halo", [16, D])
    x_allT = din("x_allT", [D, 2064])
    pos_i = din("pos", [1, S], I32)
    halo_valid = din("halo_valid", [1, 16])
    c_col = din("c_col", [128, KC])
    ident_d = din("ident", [128, 128])
    mask_own_d = din("mask_own", [128, 128])
    mask_oth_d = din("mask_oth", [128, 128])
    invf_d = din("invf_col", [128, 1])
    sgn_d = din("sgn_col", [128, 1])
    w_ada = din("w_ada", [D, 6 * D])
    b_ada_col = din("b_ada_col", [128, 96])
    w_in = din("w_in", [D, 11328])
    w_rope = din("w_rope", [D, 256])
    w_qb = din("w_qb", [512, 4096])
    w_kvb = din("w_kvb", [512, 4096])
    w_o_a = din("w_o_a", [D, D])
    w_o_b = din("w_o_b", [D, D])
    w_o = din("w_o", [D, D])
    w_ffn_in = din("w_ffn_in", [D, 2 * DFF])
    w_ffn_out = din("w_ffn_out", [DFF, D])
    gq_col = din("gq_col", [128, 4])
    gkv_col = din("gkv_col", [128, 4])
    wconv_col = din("wconv_col", [128, KC * 3])
    ln1g_col = din("ln1g_col", [128, KC])
    ln1b_col = din("ln1b_col", [128, KC])
    ln1_g = din("ln1_g", [1, D])
    ln1_b = din("ln1_b", [1, D])
    ln2_g = din("ln2_g", [1, D])
    ln2_b = din("ln2_b", [1, D])
    y = nc.dram_tensor("y", [T, D], F32, kind="ExternalOutput").ap()
    x1d = nc.dram_tensor("x1d", [T, D], F32, kind="Internal").ap()

    dump_out = {}

    ARENA = 212480
    arena = es.enter_context(nc.sbuf_tensor("arena", [128, ARENA // 4], F32))

    def view(off, shape, dt=F32):
        esz = 4 if dt in (F32, I32) else 2
        n = 1
        for s_ in shape[1:]:
            n *= s_
        nb = n * esz
        assert off % 4 == 0 and nb % 4 == 0 and off + nb <= ARENA, (off, shape)
        a = arena[:, off // 4:(off + nb) // 4]
        if dt != F32:
            a = a.bitcast(dt)
        if len(shape) == 3:
            a = a.rearrange("p (a b) -> p a b", a=shape[1], b=shape[2])
        elif len(shape) == 4:
            a = a.rearrange("p (a b c) -> p a b c", a=shape[1], b=shape[2], c=shape[3])
        return a

    ps = [es.enter_context(nc.psum_tensor("ps%d" % i, [128, 512], F32)) for i in range(8)]

    KB = 1024
    o = 0
    ident = view(o, [128, 128]); o += 512
    ident_bf = view(o, [128, 128], BF16); o += 256
    mask_own = view(o, [128, 128], BF16); o += 256
    mask_oth = view(o, [128, 128], BF16); o += 256
    ones_bf = view(o, [128, 128], BF16); o += 256
    modT = view(o, [128, 96]); o += 384
    sc1p = view(o, [128, 16]); o += 64
    sc2p = view(o, [128, 16]); o += 64
    ccol = view(o, [128, 16]); o += 64
    cact = view(o, [128, 16], BF16); o += 32
    badac = view(o, [128, 96]); o += 384
    gq = view(o, [128, 4]); o += 16
    gkv = view(o, [128, 4]); o += 16
    wconv = view(o, [128, 48]); o += 192
    hvalid = view(o, [128, 16]); o += 64
    invf = view(o, [128, 1]); o += 4
    sgn = view(o, [128, 1]); o += 4
    small = view(o, [128, 64]); o += 256
    g1c = view(o, [128, 16]); o += 64
    b1c = view(o, [128, 16]); o += 64
    G2c = view(o, [128, 16]); o += 64
    B2c = view(o, [128, 16]); o += 64
    assert o <= 4 * KB
    NSLOT = 3
    wslot = [view(4 * KB + i * 16 * KB, [128, 8192], BF16) for i in range(NSLOT)]
    wstate = {"n": 0}

    def wload(parts):
        s_ = wstate["n"] % NSLOT
        wstate["n"] += 1
        for pi, (dst_fn, src) in enumerate(parts):
            sc.dma("pool", dst_fn(wslot[s_]), src, w=[("w", s_)], sem=("w", s_), nodep=(pi > 0))
        return s_

    def wtile_std(src_cols_ap, ncols, kchunks=KC):
        def dst(sl):
            return sl[:, 0:kchunks * ncols].rearrange("p (k n) -> p k n", k=kchunks, n=ncols)
        return (dst, src_cols_ap.rearrange("(k p) n -> p k n", p=128))

    def wtile_pair(src_a, src_b):
        def dst_a(sl):
            return sl[:, 0:KC * 512].rearrange("p (k n) -> p k n", k=KC, n=512)[:, :, 0:256]

        def dst_b(sl):
            return sl[:, 0:KC * 512].rearrange("p (k n) -> p k n", k=KC, n=512)[:, :, 256:512]
        return [(dst_a, src_a.rearrange("(k p) n -> p k n", p=128)), (dst_b, src_b.rearrange("(k p) n -> p k n", p=128))]

    def wview(s_, ncols, kchunks=KC, off=0):
        return wslot[s_][:, off:off + kchunks * ncols].rearrange("p (k n) -> p k n", k=kchunks, n=ncols)

    B0 = 52 * KB
    uT_own = view(B0, [128, KC, 1040], BF16)
    O_UOTH = B0 + 33280
    uT_oth = view(O_UOTH, [128, KC, 1024], BF16)
    O_XS = O_UOTH + 32768
    xs = [view(O_XS + i * 8192, [128, 2048]) for i in range(2)]
    O_TAB = O_XS + 16384
    cosT = view(O_TAB, [128, 2048])
    sinT = view(O_TAB + 8192, [128, 2048])
    O_KVN = O_TAB + 16384
    kvnT = view(O_KVN, [128, 4, 2048], BF16)
    O_QN = O_KVN + 16384
    qnT = view(O_QN, [128, 4, 1024], BF16)
    O_KR = O_QN + 8192
    krT = view(O_KR, [128, 2048], BF16)
    O_TMP = O_KR + 4096
    assert O_TMP + 24 * KB <= ARENA

    def stop_here(name):
        return stop == name

    def add_dump(name, ap, shape, dt):
        t = nc.dram_tensor("dbg_" + name, list(shape), dt, kind="ExternalOutput").ap()
        dump_out[name] = t
        sc.barrier()
        sc.dma("sp", t, ap, sem=("dump", name))

    tmpc = view(O_TMP, [128, 128])
    tmpc2 = view(O_TMP + 512, [128, 128])
    for dst, src in [(ident, ident_d), (tmpc, mask_own_d), (tmpc2, mask_oth_d), (ccol, c_col),
                     (badac, b_ada_col), (gq, gq_col), (gkv, gkv_col), (wconv, wconv_col),
                     (invf, invf_d), (sgn, sgn_d), (g1c, ln1g_col), (b1c, ln1b_col),
                     (hvalid, halo_valid.to_broadcast([128, 16]))]:
        sc.dma("sp", dst, src, w=[("const",)], sem=("const",), nodep=True)
    CONST = ("const",)
    sc.op("dve", lambda e: e.tensor_copy(out=ident_bf, in_=ident), r=[CONST], w=[("identbf",)])
    sc.op("dve", lambda e: e.tensor_copy(out=mask_own, in_=tmpc), r=[CONST], w=[("masks",)])
    sc.op("dve", lambda e: e.tensor_copy(out=mask_oth, in_=tmpc2), r=[CONST], w=[("masks",)])
    sc.op("dve", lambda e: e.memset(ones_bf, 1.0), w=[("ones",)])
    sc.op("act", lambda e: e.activation(out=cact, in_=ccol, func=AF.Silu), r=[CONST], w=[("cact",)])

    def adaln(tiles, bank):
        for tj in tiles:
            s_ = wload([wtile_std(w_ada[:, tj * 512:(tj + 1) * 512], 512)])
            wv = wview(s_, 512)
            for mm in range(4):
                m = tj * 4 + mm
                for k in range(KC):
                    sc.op("pe", lambda e, m=m, k=k, mm=mm, wv=wv: e.matmul(
                        ps[bank][:, m:m + 1], wv[:, k, mm * 128:(mm + 1) * 128], cact[:, k:k + 1],
                        start=(k == 0), stop=(k == KC - 1)),
                        r=[("w", s_), ("cact",)], w=[("ps", bank)])
        lo, hi = tiles[0] * 4, tiles[-1] * 4 + 4
        sc.op("dve", lambda e: e.tensor_tensor(out=modT[:, lo:hi], in0=ps[bank][:, lo:hi], in1=badac[:, lo:hi],
                                               op=ALU.add), r=[("ps", bank), CONST], w=[("modT", lo)])

    posb = view(O_TMP, [128, 2048], I32)
    tA = view(O_TMP + 8192, [128, 2048])
    tB = view(O_TMP + 16384, [128, 2048])
    sc.dma("sp", posb, pos_i.to_broadcast([128, S]), r=[("masks",)], w=[("posb",)], sem=("posb",))
    sc.op("dve", lambda e: e.tensor_copy(out=tA, in_=posb), r=[("posb",)], w=[("tA",)])
    sc.op("dve", lambda e: e.tensor_scalar(out=sinT, in0=tA, scalar1=invf[:, 0:1], scalar2=None, op0=ALU.mult),
          r=[("tA",), CONST], w=[("ang",)])
    ang = sinT
    TWO_PI = 2.0 * math.pi
    C1 = 6.28125
    C2 = TWO_PI - C1
    ki = posb

    def table(dst, shift, scale_ap):
        sc.op("dve", lambda e: e.tensor_scalar(out=tA, in0=ang, scalar1=1.0 / TWO_PI, scalar2=shift / TWO_PI + 0.5,
                                               op0=ALU.mult, op1=ALU.add), r=[("ang",)], w=[("tA",)])
        sc.op("dve", lambda e: e.tensor_copy(out=ki, in_=tA), r=[("tA",)], w=[("ki",)])
        sc.op("dve", lambda e: e.tensor_copy(out=tA, in_=ki), r=[("ki",)], w=[("tA",)])
        sc.op("dve", lambda e: e.tensor_scalar(out=tB, in0=ang, scalar1=shift, scalar2=None, op0=ALU.add),
              r=[("ang",)], w=[("tB",)])
        sc.op("dve", lambda e: e.scalar_tensor_tensor(out=tB, in0=tA, scalar=-C1, in1=tB, op0=ALU.mult, op1=ALU.add),
              r=[("tA",), ("tB",)], w=[("tB",)])
        sc.op("dve", lambda e: e.scalar_tensor_tensor(out=tB, in0=tA, scalar=-C2, in1=tB, op0=ALU.mult, op1=ALU.add),
              r=[("tA",), ("tB",)], w=[("tB",)])
        sc.op("dve", lambda e: e.tensor_scalar(out=tA, in0=tB, scalar1=-math.pi, scalar2=TWO_PI, op0=ALU.is_lt, op1=ALU.mult),
              r=[("tB",)], w=[("tA",)])
        sc.op("dve", lambda e: e.tensor_tensor(out=tB, in0=tB, in1=tA, op=ALU.add), r=[("tA",), ("tB",)], w=[("tB",)])
        sc.op("dve", lambda e: e.tensor_scalar(out=tA, in0=tB, scalar1=math.pi, scalar2=-TWO_PI, op0=ALU.is_gt, op1=ALU.mult),
              r=[("tB",)], w=[("tA",)])
        sc.op("dve", lambda e: e.tensor_tensor(out=tB, in0=tB, in1=tA, op=ALU.add), r=[("tA",), ("tB",)], w=[("tB",)])
        sc.op("dve", lambda e: e.tensor_scalar(out=tB, in0=tB, scalar1=-math.pi, scalar2=math.pi, op0=ALU.max, op1=ALU.min),
              r=[("tB",)], w=[("tB",)])
        if scale_ap is None:
            sc.op("act", lambda e: e.activation(out=dst, in_=tB, func=AF.Sin), r=[("tB",)], w=[("tab", id(dst))])
        else:
            sc.op("act", lambda e: e.activation(out=dst, in_=tB, func=AF.Sin, scale=scale_ap),
                  r=[("tB",), CONST], w=[("tab", id(dst))])

    table(cosT, math.pi / 2.0, None)
    table(sinT, 0.0, sgn[:, 0:1])
    TABS = [("tab", id(cosT)), ("tab", id(sinT))]

    adaln(list(range(0, 8)), 0)
    sc.op("dve", lambda e: e.tensor_scalar(out=sc1p, in0=modT[:, 16:32], scalar1=1.0, scalar2=None, op0=ALU.add),
          r=[("modT", 0)], w=[("mod1",)])
    MOD1 = ("mod1",)

    stgA = [view(O_XS + i * 4160, [128, 1040]) for i in range(2)] + [view(O_KVN + i * 4160, [128, 1040]) for i in range(6)]
    assert O_KVN + 6 * 4160 <= O_TMP
    stgB = [view(O_XS + 8320, [128, 1024]), view(O_TMP + 24 * KB, [128, 1024])]
    assert O_XS + 8320 + 4096 <= O_TAB and O_TMP + 24 * KB + 4096 <= ARENA

    def u_piece(which, m):
        if which == "own":
            sb = stgA[m % 8]
            sk = ("stgA", m % 8)
            src_d = x_allT[m * 128:(m + 1) * 128, 0:1040]
            dst = uT_own[:, m, 0:1040]
            keys = [("uT", "own", 0), ("uT", "own", 1), ("uT", "halo")]
        else:
            sb = stgB[m % 2]
            sk = ("stgB", m % 2)
            src_d = x_allT[m * 128:(m + 1) * 128, 1040:2064]
            dst = uT_oth[:, m, :]
            keys = [("uT", "oth", 0), ("uT", "oth", 1)]
        sc.dma("sp", sb, src_d, w=[sk], sem=sk)
        if m % 2 == 0:
            sc.op("act", lambda e: e.activation(
                out=dst, in_=sb, func=AF.Identity, bias=modT[:, m:m + 1], scale=sc1p[:, m:m + 1]),
                r=[sk, MOD1, ("modT", 0)], pw=keys)
        else:
            sc.op("dve", lambda e: e.tensor_scalar(
                out=dst, in0=sb, scalar1=sc1p[:, m:m + 1], scalar2=modT[:, m:m + 1],
                op0=ALU.mult, op1=ALU.add),
                r=[sk, MOD1, ("modT", 0)], pw=keys)

    for m in range(KC):
        u_piece("own", m)
    oth_todo = list(range(KC))

    def slip_oth(n=1):
        for _ in range(n):
            if oth_todo:
                u_piece("oth", oth_todo.pop(0))

    if stop_here("p1"):
        add_dump("uT_own", uT_own, [128, KC, 1040], BF16)
        add_dump("uT_oth", uT_oth, [128, KC, 1024], BF16)
        add_dump("modT", modT, [128, 96], F32)
        return finish(nc, sc, es, dump_out)

    s_kv = wload([wtile_std(w_in[:, 512:1024], 512)])
    s_r = wload([wtile_std(w_rope[:, 0:256], 256)])
    s_q = wload([wtile_std(w_in[:, 0:512], 512)])
    wkv, wq, wr = wview(s_kv, 512), wview(s_q, 512), wview(s_r, 256)
    a32 = [view(O_TMP + i * 2048, [128, 512]) for i in range(4)]
    sqb = [view(O_TMP + 8192 + i * 1024, [128, 512], BF16) for i in range(4)]
    rstd = view(O_TMP + 12288, [128, 512])
    rtmp = view(O_TMP + 14336, [128, 512])
    rtmp2 = view(O_TMP + 16384, [128, 512])
    bank_rr = {"n": 0}

    def nbank(pool):
        b = pool[bank_rr["n"] % len(pool)]
        bank_rr["n"] += 1
        return b

    def ugrp(g):
        if g < 2:
            return (lambda k: uT_own[:, k, g * 512:(g + 1) * 512]), ("uT", "own", g)
        return (lambda k: uT_oth[:, k, (g - 2) * 512:(g - 1) * 512]), ("uT", "oth", g - 2)

    def latent_norm(g, wv, wkey, gcol, outT, okey, ssq_bank, tag):
        uf, ukey = ugrp(g)
        for m in range(4):
            b = nbank([0, 1, 2, 3, 4, 5])
            for k in range(KC):
                sc.op("pe", lambda e, b=b, m=m, k=k: e.matmul(ps[b][:, :], wv[:, k, m * 128:(m + 1) * 128], uf(k),
                                                              start=(k == 0), stop=(k == KC - 1)),
                      r=[wkey, ukey], w=[("ps", b)])
            sc.op("act", lambda e, b=b, m=m: e.activation(out=a32[m], in_=ps[b][:, :], func=AF.Copy),
                  r=[("ps", b)] + TABS, w=[("a32", m)])
            sc.op("act", lambda e, b=b, m=m: e.activation(out=sqb[m], in_=ps[b][:, :], func=AF.Square),
                  r=[("ps", b)] + TABS, w=[("sqb", m)])
            slip_oth()
        for m in range(4):
            sc.op("pe", lambda e, m=m: e.matmul(ps[ssq_bank][:, :], ones_bf, sqb[m], start=(m == 0), stop=(m == 3)),
                  r=[("sqb", m), ("ones",)], w=[("ps", ssq_bank)])
        sc.op("dve", lambda e: e.tensor_scalar(out=rtmp, in0=ps[ssq_bank][:, :], scalar1=1.0 / 512.0, scalar2=RMS_EPS,
                                               op0=ALU.mult, op1=ALU.add), r=[("ps", ssq_bank)] + TABS, w=[("rtmp",)])
        sc.op("act", lambda e: e.activation(out=rtmp, in_=rtmp, func=AF.Sqrt), r=[("rtmp",)], w=[("rtmp",)])
        sc.op("dve", lambda e: e.reciprocal(out=rstd, in_=rtmp), r=[("rtmp",)], w=[("rstd",)])
        for m in range(4):
            sc.op("dve", lambda e, m=m: e.scalar_tensor_tensor(
                out=outT[:, m, g * 512:(g + 1) * 512], in0=a32[m], scalar=gcol[:, m:m + 1], in1=rstd,
                op0=ALU.mult, op1=ALU.mult), r=[("a32", m), ("rstd",), CONST], w=[okey(g)])

    for g in range(4):
        latent_norm(g, wkv, ("w", s_kv), gkv, kvnT, lambda g_: ("kvnT", g_), 6, "kv")
        uf, ukey = ugrp(g)
        bA = nbank([0, 1, 2, 3, 4, 5])
        bB = nbank([0, 1, 2, 3, 4, 5])
        for (b, c0) in ((bA, 0), (bB, 128)):
            for k in range(KC):
                sc.op("pe", lambda e, b=b, c0=c0, k=k, uf=uf: e.matmul(ps[b][:, :], wr[:, k, c0:c0 + 128], uf(k),
                                                                start=(k == 0), stop=(k == KC - 1)),
                      r=[("w", s_r), ukey], w=[("ps", b)])
        cs = slice(g * 512, (g + 1) * 512)
        sc.op("dve", lambda e, bA=bA, cs=cs: e.tensor_tensor(out=rtmp2, in0=ps[bA][:, :], in1=cosT[:, cs], op=ALU.mult),
              r=[("ps", bA)] + TABS, w=[("rtmp2",)])
        sc.op("dve", lambda e, bB=bB, cs=cs: e.tensor_tensor(out=rtmp, in0=ps[bB][:, :], in1=sinT[:, cs], op=ALU.mult),
              r=[("ps", bB)] + TABS, w=[("rtmp",)])
        sc.op("dve", lambda e, cs=cs: e.tensor_tensor(out=krT[:, cs], in0=rtmp, in1=rtmp2, op=ALU.add),
              r=[("rtmp",), ("rtmp2",)], w=[("krT", g)])
        if g < 2:
            latent_norm(g, wq, ("w", s_q), gq, qnT, lambda g_: ("qnT", g_), 7, "q")
        while g == 1 and oth_todo:
            slip_oth()

    if stop_here("p2"):
        add_dump("kvnT", kvnT, [128, 4, 2048], BF16)
        add_dump("qnT", qnT, [128, 4, 1024], BF16)
        add_dump("krT", krT, [128, 2048], BF16)
        add_dump("cosT", cosT, [128, 2048], F32)
        add_dump("sinT", sinT, [128, 2048], F32)
        return finish(nc, sc, es, dump_out)

    sc.barrier()
    attnT = view(O_UOTH, [128, NH, 1024], BF16)
    O_A = O_XS
    KT = view(O_A, [128, 2, 2048], BF16)
    QT = view(O_A + 8192, [128, 2, 1024], BF16)
    qrT = view(O_A + 12288, [128, 1024], BF16)
    Osb = view(O_A + 14336, [128, 8, 128], BF16)
    O_B = O_TMP
    Vaug = view(O_B, [128, 16, 2, 130], BF16)
    PT = [view(O_B + 8320 + i * 1024, [128, 512], BF16) for i in range(6)]
    qtmp = view(O_B + 8320 + 6144, [128, 512])
    qtmp2 = view(O_B + 8320 + 8192, [128, 512])
    rcp = view(O_B + 8320 + 10240, [128, 8])
    qrTz = [view(O_A + 12288, [128, 1024], BF16), view(O_B + 8320 + 10240 + 64, [128, 1024], BF16)]
    assert O_B + 8320 + 10240 + 64 + 2048 <= ARENA
    sc.op("dve", lambda e: e.memset(qrTz[0][64:128, :], 0.0), w=[("qrT", 0)])
    sc.op("dve", lambda e: e.memset(qrTz[1][0:64, :], 0.0), w=[("qrT", 1)])
    sc.op("dve", lambda e: e.memset(Vaug[:, :, :, 128:130], 1.0), w=[("Vones",)])
    PROJ_BANKS = [3, 4, 5, 6, 7]
    O_BANK = [0, 1, 2]
    ST_BANKS = [3, 4, 5, 6]
    st_n = {"n": 0}
    TR_BANK = 7
    evac_rr = {"n": 0}

    def evac_copy(dst, src, r, w):
        evac_rr["n"] += 1
        if evac_rr["n"] % 2:
            sc.op("act", lambda e: e.activation(out=dst, in_=src, func=AF.Copy), r=r, pw=w)
        else:
            sc.op("dve", lambda e: e.tensor_copy(out=dst, in_=src), r=r, pw=w)

    pt_n = {"n": 0}
    deferred = []
    ada_rest = list(range(8, 24))
    for hp in range(8):
        def part(off, ncols, src):
            def dst(sl):
                return sl[:, off:off + 4 * ncols].rearrange("p (k n) -> p k n", k=4, n=ncols)
            return (dst, src.rearrange("(k p) n -> p k n", p=128))
        s_ = wload([
            part(0, 256, w_kvb[:, hp * 256:(hp + 1) * 256]),
            part(1024, 256, w_kvb[:, 2048 + hp * 256:2048 + (hp + 1) * 256]),
            part(2048, 256, w_qb[:, hp * 256:(hp + 1) * 256]),
            part(3072, 128, w_qb[:, 2048 + hp * 128:2048 + (hp + 1) * 128]),
            part(3584, 128, w_qb[:, 3072 + hp * 128:3072 + (hp + 1) * 128]),
        ])
        wk_ = wview(s_, 256, 4, 0)
        wv_ = wview(s_, 256, 4, 1024)
        wqn_ = wview(s_, 256, 4, 2048)
        wqr_ = wview(s_, 128, 4, 3072)
        wqs_ = wview(s_, 128, 4, 3584)
        WK = ("w", s_)
        for h in range(2):
            for n in range(4):
                b = nbank(PROJ_BANKS)
                for k in range(4):
                    sc.op("pe", lambda e, b=b, h=h, n=n, k=k, wk_=wk_: e.matmul(
                        ps[b][:, :], wk_[:, k, h * 128:(h + 1) * 128], kvnT[:, k, n * 512:(n + 1) * 512],
                        start=(k == 0), stop=(k == 3)), r=[WK, ("kvnT", n)], w=[("ps", b)])
                evac_copy(KT[:, h, n * 512:(n + 1) * 512], ps[b][:, :], [("ps", b)], [("KT", h)])
        for tp in range(8):
            b = nbank(PROJ_BANKS)
            for half in range(2):
                tt = tp * 2 + half
                for k in range(4):
                    sc.op("pe", lambda e, b=b, tt=tt, half=half, k=k, wv_=wv_: e.matmul(
                        ps[b][:, half * 256:(half + 1) * 256], kvnT[:, k, tt * 128:(tt + 1) * 128], wv_[:, k, :],
                        start=(k == 0 and half == 0), stop=(k == 3), skip_group_check=True),
                        r=[WK, ("kvnT", tt // 4)], w=[("ps", b)])
            src = ps[b][:, :].rearrange("p (t h d) -> p t h d", t=2, h=2, d=128)
            evac_copy(Vaug[:, tp * 2:tp * 2 + 2, :, 0:128], src, [("ps", b)], [("V",)])
        for h in range(2):
            for n in range(2):
                b = nbank(PROJ_BANKS)
                for k in range(4):
                    sc.op("pe", lambda e, b=b, h=h, n=n, k=k, wqn_=wqn_: e.matmul(
                        ps[b][:, :], wqn_[:, k, h * 128:(h + 1) * 128], qnT[:, k, n * 512:(n + 1) * 512],
                        start=(k == 0), stop=(k == 3)), r=[WK, ("qnT", n)], w=[("ps", b)])
                evac_copy(QT[:, h, n * 512:(n + 1) * 512], ps[b][:, :], [("ps", b)], [("QT", h)])
        for n in range(2):
            bA = nbank(PROJ_BANKS)
            bB = nbank(PROJ_BANKS)
            for (b, wv2) in ((bA, wqr_), (bB, wqs_)):
                for k in range(4):
                    sc.op("pe", lambda e, b=b, n=n, k=k, wv2=wv2: e.matmul(
                        ps[b][:, :], wv2[:, k, :], qnT[:, k, n * 512:(n + 1) * 512],
                        start=(k == 0), stop=(k == 3)), r=[WK, ("qnT", n)], w=[("ps", b)])
            cs = slice(n * 512, (n + 1) * 512)
            sc.op("dve", lambda e, bA=bA, cs=cs: e.tensor_tensor(out=qtmp, in0=ps[bA][:, :], in1=cosT[:, cs], op=ALU.mult),
                  r=[("ps", bA)] + TABS, w=[("qtmp",)])
            sc.op("dve", lambda e, bB=bB, cs=cs: e.tensor_tensor(out=qtmp2, in0=ps[bB][:, :], in1=sinT[:, cs], op=ALU.mult),
                  r=[("ps", bB)] + TABS, w=[("qtmp2",)])
            for hh in range(2):
                rp_ = slice(hh * 64, (hh + 1) * 64)
                sc.op("dve", lambda e, cs=cs, hh=hh, rp_=rp_: e.tensor_tensor(
                    out=qrTz[hh][rp_, cs], in0=qtmp[rp_, :], in1=qtmp2[rp_, :], op=ALU.add),
                    r=[("qtmp",), ("qtmp2",)], w=[("qrT", hh)])
        for h in range(2):
            hg = hp * 2 + h
            rp = slice(h * 64, (h + 1) * 64)
            units = []
            for c in range(16):
                i = c // 2
                nq = (8 - i) * 128
                for (p0, pw) in [(0, min(512, nq))] + ([(512, nq - 512)] if nq > 512 else []):
                    units.append((c, i, p0, pw))

            def emit_st(u, uinfo):
                c, i, p0, pw = uinfo
                own = (c % 2 == 0)
                tt = i if own else 8 + i
                kcs = slice(tt * 128, (tt + 1) * 128)
                q0 = i * 128
                b = ST_BANKS[st_n["n"] % len(ST_BANKS)]
                st_n["n"] += 1
                slot = pt_n["n"] % len(PT)
                pt_n["n"] += 1
                qs = slice(q0 + p0, q0 + p0 + pw)
                sc.op("pe", lambda e, b=b, pw=pw, kcs=kcs, qs=qs, h=h: e.matmul(
                    ps[b][:, 0:pw], KT[:, h, kcs], QT[:, h, qs], start=True, stop=False),
                    r=[("KT", h), ("QT", h)], w=[("ps", b)])
                sc.op("pe", lambda e, b=b, pw=pw, kcs=kcs, qs=qs, p0=p0, h=h: e.matmul(
                    ps[b][:, 0:pw], krT[:, kcs], qrTz[h][:, qs], start=False, stop=(p0 != 0)),
                    r=[("krT", tt // 4), ("qrT", h)], w=[("ps", b)])
                if p0 == 0:
                    mk = mask_own if own else mask_oth
                    sc.op("pe", lambda e, b=b, mk=mk: e.matmul(
                        ps[b][:, 0:128], ident_bf, mk, start=False, stop=True),
                        r=[("identbf",), ("masks",)], w=[("ps", b)])
                sc.op("act", lambda e, b=b, pw=pw, slot=slot: e.activation(
                    out=PT[slot][:, 0:pw], in_=ps[b][:, 0:pw], func=AF.Exp, scale=ATTN_SCALE),
                    r=[("ps", b)], w=[("PT", slot)])
                return (c, i, tt, slot, p0, pw)

            def emit_pv(info):
                c, i, tt, slot, p0, pw = info
                for jj in range(pw // 128):
                    j = i + p0 // 128 + jj
                    ob = O_BANK[j // 3]
                    oc = (j % 3) * 130
                    sc.op("pe", lambda e, ob=ob, oc=oc, j=j, jj=jj, tt=tt, slot=slot, c=c, h=h: e.matmul(
                        ps[ob][:, oc:oc + 129], PT[slot][:, jj * 128:(jj + 1) * 128], Vaug[:, tt, h, 0:129],
                        start=(c == 0 and j % 3 == 0), stop=(c == 2 * j + 1), skip_group_check=True),
                        r=[("PT", slot), ("V",), ("Vones",)], w=[("ps", ob)])

            LOOK = 2
            infos = []
            for u, uinfo in enumerate(units):
                infos.append(emit_st(u, uinfo))
                if u == 2 and deferred:
                    deferred.pop()()
                if u >= LOOK:
                    emit_pv(infos[u - LOOK])
            for u in range(len(units) - LOOK, len(units)):
                emit_pv(infos[u])
            for ob in range(3):
                nj = 3 if ob < 2 else 2
                src = ps[O_BANK[ob]][:, 0:nj * 130].rearrange("p (j d) -> p j d", j=nj, d=130)
                sc.op("dve", lambda e, ob=ob, nj=nj, src=src: e.reciprocal(
                    out=rcp[:, ob * 3:ob * 3 + nj].rearrange("p (j o) -> p j o", o=1), in_=src[:, :, 128:129]),
                    r=[("ps", O_BANK[ob])], pw=[("rcp",)])
            for j in range(8):
                ob = O_BANK[j // 3]
                oc = (j % 3) * 130
                if (j // 3) % 2 == 0:
                    sc.op("act", lambda e, ob=ob, oc=oc, j=j: e.activation(
                        out=Osb[:, j, :], in_=ps[ob][:, oc:oc + 128], func=AF.Identity, scale=rcp[:, j:j + 1]),
                        r=[("ps", ob), ("rcp",)], pw=[("Osb",)])
                else:
                    sc.op("dve", lambda e, ob=ob, oc=oc, j=j: e.tensor_scalar(
                        out=Osb[:, j, :], in0=ps[ob][:, oc:oc + 128], scalar1=rcp[:, j:j + 1], scalar2=None, op0=ALU.mult),
                        r=[("ps", ob), ("rcp",)], pw=[("Osb",)])
            def fin_tr(hg=hg):
                trv = ps[TR_BANK][:, :].bitcast(BF16)
                for j in range(8):
                    sc.op("pe", lambda e, j=j, trv=trv: e.transpose(trv[:, j * 128:(j + 1) * 128], Osb[:, j, :], ident_bf),
                          r=[("Osb",), ("identbf",)], w=[("ps", TR_BANK)])
                sc.op("dve", lambda e, hg=hg, trv=trv: e.tensor_copy(out=attnT[:, hg, :], in_=trv),
                      r=[("ps", TR_BANK)], w=[("attnT", hg)])
            deferred.append(fin_tr)
        adaln(ada_rest[hp * 2:hp * 2 + 2], 7)
    while deferred:
        deferred.pop()()
    sc.op("dve", lambda e: e.tensor_scalar(out=sc2p, in0=modT[:, 64:80], scalar1=1.0, scalar2=None, op0=ALU.add),
          r=[("modT", 64), ("modT", 72)], w=[("mod2",)])

    def conv_parts(m):
        def part(off, src):
            def dst(sl):
                return sl[:, off:off + KC * 128].rearrange("p (k n) -> p k n", k=KC, n=128)
            return (dst, src.rearrange("(k p) n -> p k n", p=128))
        return [part(0, w_in[:, 3136 + m * 128:3136 + (m + 1) * 128]),
                part(2048, w_in[:, 5184 + m * 128:5184 + (m + 1) * 128]),
                part(4096, w_in[:, 1088 + m * 128:1088 + (m + 1) * 128])]
    pre_conv = [wload(conv_parts(0)), wload(conv_parts(1))] if stop is None or stop not in ("p3",) else []

    if stop_here("p3"):
        add_dump("attnT", attnT, [128, NH, 1024], BF16)
        add_dump("modT", modT, [128, 96], F32)
        add_dump("kvnT", kvnT, [128, 4, 2048], BF16)
        add_dump("qnT", qnT, [128, 4, 1024], BF16)
        add_dump("krT", krT, [128, 2048], BF16)
        add_dump("cosT", cosT, [128, 2048], F32)
        add_dump("sinT", sinT, [128, 2048], F32)
        return finish(nc, sc, es, dump_out)

    ALIAS3 = ([("KT", 0), ("KT", 1), ("QT", 0), ("QT", 1), ("qrT", 0), ("qrT", 1), ("Osb",), ("V",), ("Vones",),
               ("qtmp",), ("qtmp2",), ("rcp",)] + [("PT", i_) for i_ in range(6)] + [("kvnT", i_) for i_ in range(4)]
              + [("qnT", 0), ("qnT", 1)] + [("krT", i_) for i_ in range(4)] + TABS)
    sc.op("act", lambda e: e.activation(out=small[:, 60:61], in_=small[:, 60:61], func=AF.Copy), w=ALIAS3 + [("fence4", "a")])
    sc.op("dve", lambda e: e.tensor_copy(out=small[:, 61:62], in_=small[:, 61:62]), w=ALIAS3 + [("fence4", "d")])
    MODALL = [("modT", c_) for c_ in range(32, 96, 8)]
    O_C = O_XS
    mbT = view(O_C, [128, KC, 1024], BF16)
    ybT = view(O_C + 32768, [128, KC, 1024], BF16)
    O_CT = O_C + 65536
    cbuf = view(O_CT, [128, 1040])
    zbuf = view(O_CT + 4160, [128, 8, 130])
    cacc = view(O_CT + 8320, [128, 8, 128])
    sgt = view(O_CT + 12416, [128, 1024])
    assert O_CT + 16512 <= ARENA
    uO = ("uT", "own", 0), ("uT", "own", 1), ("uT", "halo")
    for m in range(KC):
        s_ = pre_conv[m] if m < len(pre_conv) else wload(conv_parts(m))
        wc_, wx_, wb_ = wview(s_, 128, KC, 0), wview(s_, 128, KC, 2048), wview(s_, 128, KC, 4096)
        WK = ("w", s_)
        for (wv2, b0, hc) in ((wc_, 0, 0), (wx_, 2, 16), (wb_, 4, None)):
            for k in range(KC):
                for n in range(2):
                    sc.op("pe", lambda e, wv2=wv2, b0=b0, n=n, k=k: e.matmul(
                        ps[b0 + n][:, :], wv2[:, k, :], uT_own[:, k, n * 512:(n + 1) * 512],
                        start=(k == 0), stop=(k == KC - 1)), r=[WK, ("uT", "own", n)], w=[("ps", b0 + n)])
                if hc is not None:
                    sc.op("pe", lambda e, wv2=wv2, hc=hc, k=k: e.matmul(
                        ps[6][:, hc:hc + 16], wv2[:, k, :], uT_own[:, k, 1024:1040],
                        start=(k == 0), stop=(k == KC - 1)), r=[WK, ("uT", "halo")], w=[("ps", 6)])
            if b0 == 0:
                for n in range(2):
                    sc.op("act", lambda e, n=n: e.activation(out=cbuf[:, n * 512:(n + 1) * 512], in_=ps[n][:, :], func=AF.Copy),
                          r=[("ps", n)], w=[("cbuf",)])
                sc.op("act", lambda e: e.activation(out=cbuf[:, 1024:1040], in_=ps[6][:, 0:16], func=AF.Copy),
                      r=[("ps", 6)], w=[("cbuf",)])
            elif b0 == 2:
                for n in range(2):
                    sc.op("dve", lambda e, n=n: e.tensor_tensor(
                        out=zbuf[:, n * 4:(n + 1) * 4, 2:130],
                        in0=ps[2 + n][:, :].rearrange("p (j t) -> p j t", j=4, t=128),
                        in1=cbuf[:, n * 512:(n + 1) * 512].rearrange("p (j t) -> p j t", j=4, t=128), op=ALU.mult),
                        r=[("ps", 2 + n), ("cbuf",)], w=[("zbuf",)])
                sc.op("dve", lambda e: e.tensor_tensor(
                    out=zbuf[:, :, 0:2], in0=ps[6][:, 16:32].rearrange("p (j t) -> p j t", j=8, t=2),
                    in1=cbuf[:, 1024:1040].rearrange("p (j t) -> p j t", j=8, t=2), op=ALU.mult),
                    r=[("ps", 6), ("cbuf",)], w=[("zbuf",)])
                sc.op("dve", lambda e: e.tensor_tensor(
                    out=zbuf[:, :, 0:2], in0=zbuf[:, :, 0:2], in1=hvalid.rearrange("p (j t) -> p j t", j=8, t=2), op=ALU.mult),
                    r=[("zbuf",), CONST], w=[("zbuf",)])
                sc.op("dve", lambda e, m=m: e.tensor_scalar(
                    out=cacc, in0=zbuf[:, :, 0:128], scalar1=wconv[:, m * 3:m * 3 + 1], scalar2=None, op0=ALU.mult),
                    r=[("zbuf",), CONST], w=[("cacc",)])
                for kk in (1, 2):
                    sc.op("dve", lambda e, m=m, kk=kk: e.scalar_tensor_tensor(
                        out=cacc, in0=zbuf[:, :, kk:kk + 128], scalar=wconv[:, m * 3 + kk:m * 3 + kk + 1], in1=cacc,
                        op0=ALU.mult, op1=ALU.add), r=[("zbuf",), ("cacc",), CONST], w=[("cacc",)])
            else:
                for n in range(2):
                    sc.op("dve", lambda e, n=n, m=m: e.tensor_tensor(
                        out=ybT[:, m, n * 512:(n + 1) * 512], in0=ps[4 + n][:, :],
                        in1=cacc[:, n * 4:(n + 1) * 4, :].rearrange("p j t -> p (j t)"), op=ALU.mult),
                        r=[("ps", 4 + n), ("cacc",)], w=[("ybT", m)])

    def gated_proj(w_main, src_act, src_keys, gate_col0, out_fn, tagk):
        for tp in range(8):
            s_w = wload(wtile_pair(w_in[:, gate_col0 + tp * 256:gate_col0 + (tp + 1) * 256],
                                   w_main[:, tp * 256:(tp + 1) * 256]))
            wv_ = wview(s_w, 512)
            for mm in range(2):
                m = tp * 2 + mm
                par = (m % 2) * 4
                for k in range(KC):
                    for n in range(2):
                        sc.op("pe", lambda e, par=par, n=n, k=k, mm=mm, wv_=wv_: e.matmul(
                            ps[par + n][:, :], wv_[:, k, mm * 128:(mm + 1) * 128], uT_own[:, k, n * 512:(n + 1) * 512],
                            start=(k == 0), stop=(k == KC - 1)), r=[("w", s_w), ("uT", "own", n)], w=[("ps", par + n)])
                for n in range(2):
                    sc.op("act", lambda e, par=par, n=n: e.activation(
                        out=sgt[:, n * 512:(n + 1) * 512], in_=ps[par + n][:, :], func=AF.Sigmoid),
                        r=[("ps", par + n)], w=[("sgt", n)])
                for k in range(KC):
                    for n in range(2):
                        sc.op("pe", lambda e, par=par, n=n, k=k, mm=mm, wv_=wv_: e.matmul(
                            ps[par + 2 + n][:, :], wv_[:, k, 256 + mm * 128:256 + (mm + 1) * 128], src_act[:, k, n * 512:(n + 1) * 512],
                            start=(k == 0), stop=(k == KC - 1)), r=[("w", s_w)] + src_keys, w=[("ps", par + 2 + n)])
                for n in range(2):
                    out_fn(m, n, ps[par + 2 + n][:, :], ("ps", par + 2 + n))

    def out_b(m, n, psrc, pkey):
        sc.op("dve", lambda e: e.tensor_tensor(out=mbT[:, m, n * 512:(n + 1) * 512], in0=psrc,
                                               in1=sgt[:, n * 512:(n + 1) * 512], op=ALU.mult),
              r=[pkey, ("sgt", n)], w=[("mbT", m)])

    gated_proj(w_o_b, ybT, [("ybT", m_) for m_ in range(KC)], 9280, out_b, "b")

    def out_a(m, n, psrc, pkey):
        sc.op("dve", lambda e: e.tensor_tensor(out=sgt[:, n * 512:(n + 1) * 512], in0=psrc,
                                               in1=sgt[:, n * 512:(n + 1) * 512], op=ALU.mult),
              r=[pkey, ("sgt", n)], w=[("sgt", n)])
        sc.op("dve", lambda e: e.tensor_tensor(out=mbT[:, m, n * 512:(n + 1) * 512], in0=sgt[:, n * 512:(n + 1) * 512],
                                               in1=mbT[:, m, n * 512:(n + 1) * 512], op=ALU.add),
              r=[("sgt", n), ("mbT", m)], w=[("mbT", m)])

    gated_proj(w_o_a, attnT, [("attnT", h_) for h_ in range(NH)], 7232, out_a, "a")
    mergedT = mbT

    pre_wo = [wload([wtile_std(w_o[:, tg * 512:(tg + 1) * 512], 512)]) for tg in range(2)]

    if stop_here("p5"):
        add_dump("mergedT", mergedT, [128, KC, 1024], BF16)
        add_dump("ybT", ybT, [128, KC, 1024], BF16)
        return finish(nc, sc, es, dump_out)

    sc.barrier()
    r1buf = view(B0, [128, 8, 2048])
    assert B0 + 65536 <= O_C
    O_M = O_C + 32768
    O_U2 = ARENA - 32768
    gbc = view(O_M, [128, 2048])
    bbc = view(O_M + 8192, [128, 2048])
    mixbuf = view(O_M + 16384, [128, 2, 1024], BF16)
    stats1 = view(O_M + 20480, [128, 8, 8, 6])
    mv1 = view(O_M + 22016, [128, 8, 2])
    sd1 = view(O_M + 22080, [128, 8])
    rs1 = view(O_M + 22112, [128, 8])
    nm1 = view(O_M + 22144, [128, 8])
    assert O_M + 22176 <= O_U2
    u2T = view(O_U2, [128, KC, 1024], BF16)
    sc.dma("sp", gbc, ln1_g.to_broadcast([128, D]), w=[("gbc",)], sem=("gbc",))
    sc.dma("sp", bbc, ln1_b.to_broadcast([128, D]), w=[("bbc",)], sem=("bbc",))
    for tt in range(8):
        sc.dma("sp", r1buf[:, tt, :], x_all[tt * 128:(tt + 1) * 128, :], w=[("r1ld",)], sem=("r1ld",), nodep=(tt > 0))
    R1LD = ("r1ld",)
    sc.op("dve", lambda e: e.tensor_tensor(out=G2c, in0=g1c, in1=sc2p, op=ALU.mult), r=[CONST, ("mod2",)], w=[("G2c",)])
    sc.op("dve", lambda e: e.tensor_tensor(out=B2c, in0=b1c, in1=sc2p, op=ALU.mult), r=[CONST, ("mod2",)], w=[("B2c",)])
    sc.op("dve", lambda e: e.tensor_tensor(out=B2c, in0=B2c, in1=modT[:, 48:64], op=ALU.add), r=[("B2c",)] + MODALL, w=[("B2c",)])

    def mix_transposes(mp):
        for tt in range(8):
            bk = 6 + tt // 4
            trv = ps[bk][:, :].bitcast(BF16)
            for mo in range(2):
                c0 = (tt % 4) * 256 + mo * 128
                sc.op("pe", lambda e, trv=trv, c0=c0, mo=mo, tt=tt: e.transpose(
                    trv[:, c0:c0 + 128], mixbuf[:, mo, tt * 128:(tt + 1) * 128], ident_bf),
                    r=[("mixbuf", mo), ("identbf",)], w=[("ps", bk)])
        for tt in range(8):
            bk = 6 + tt // 4
            trv = ps[bk][:, :].bitcast(BF16)
            c0 = (tt % 4) * 256
            dstv = r1buf[:, tt, mp * 256:(mp + 1) * 256]
            sc.op("dve", lambda e, trv=trv, c0=c0, dstv=dstv: e.scalar_tensor_tensor(
                out=dstv, in0=dstv, scalar=ALPHA, in1=trv[:, c0:c0 + 256], op0=ALU.mult, op1=ALU.add),
                r=[("ps", bk), R1LD], pw=[("r1", tt)])
            sc.op("dve", lambda e, dstv=dstv, tt=tt, mp=mp: e.bn_stats(out=stats1[:, tt, mp, :], in_=dstv),
                  r=[("r1", tt)], pw=[("st1", tt)])

    pending_tr = None
    for tg in range(4):
        s_m = pre_wo[tg] if tg < len(pre_wo) else wload([wtile_std(w_o[:, tg * 512:(tg + 1) * 512], 512)])
        if tg == 3:
            pre_ffn = [wload(wtile_pair(w_ffn_in[:, tp_ * 256:(tp_ + 1) * 256], w_ffn_in[:, DFF + tp_ * 256:DFF + (tp_ + 1) * 256]))
                       for tp_ in range(2)]
        wm_ = wview(s_m, 512)
        for pr in range(2):
            mp = tg * 2 + pr

            def bank(mo, n, mp=mp):
                return (4 * mp + 2 * mo + n) % 6
            for mo in range(2):
                mm = pr * 2 + mo
                for k in range(KC):
                    for n in range(2):
                        b_ = bank(mo, n)
                        sc.op("pe", lambda e, b_=b_, n=n, k=k, mm=mm, wm_=wm_: e.matmul(
                            ps[b_][:, :], wm_[:, k, mm * 128:(mm + 1) * 128], mergedT[:, k, n * 512:(n + 1) * 512],
                            start=(k == 0), stop=(k == KC - 1)), r=[("w", s_m), ("mbT", k)], w=[("ps", b_)])
                if mo == 0 and pending_tr is not None:
                    mix_transposes(pending_tr)
                    pending_tr = None
            for mo in range(2):
                m = mp * 2 + mo
                for n in range(2):
                    b_ = bank(mo, n)
                    if n == 0:
                        sc.op("act", lambda e, b_=b_, n=n, m=m, mo=mo: e.activation(
                            out=mixbuf[:, mo, n * 512:(n + 1) * 512], in_=ps[b_][:, :], func=AF.Identity, scale=modT[:, 32 + m:33 + m]),
                            r=[("ps", b_)] + MODALL, pw=[("mixbuf", mo)])
                    else:
                        sc.op("dve", lambda e, b_=b_, n=n, m=m, mo=mo: e.tensor_scalar(
                            out=mixbuf[:, mo, n * 512:(n + 1) * 512], in0=ps[b_][:, :], scalar1=modT[:, 32 + m:33 + m],
                            scalar2=None, op0=ALU.mult), r=[("ps", b_)] + MODALL, pw=[("mixbuf", mo)])
            pending_tr = mp
    mix_transposes(pending_tr)
    for tt in range(8):
        sc.op("dve", lambda e, tt=tt: e.bn_aggr(out=mv1[:, tt, :], in_=stats1[:, tt, :, :].rearrange("p a b -> p (a b)")),
              r=[("st1", tt)], w=[("mv1",)])
    sc.op("dve", lambda e: e.tensor_scalar(out=sd1, in0=mv1[:, :, 1], scalar1=LN_EPS, scalar2=None, op0=ALU.add),
          r=[("mv1",)], w=[("sd1",)])
    sc.op("act", lambda e: e.activation(out=sd1, in_=sd1, func=AF.Sqrt), r=[("sd1",)], w=[("sd1",)])
    sc.op("dve", lambda e: e.reciprocal(out=rs1, in_=sd1), r=[("sd1",)], w=[("rs1",)])
    sc.op("dve", lambda e: e.scalar_tensor_tensor(out=nm1, in0=mv1[:, :, 0], scalar=-1.0, in1=rs1, op0=ALU.mult, op1=ALU.mult),
          r=[("mv1",), ("rs1",)], w=[("nm1",)])
    NF = DFF // 128
    actT = view(B0, [128, NF, 1024], BF16)
    O_G = B0 + NF * 2048
    sgt2 = view(O_G, [128, 1024])
    assert O_G + 4096 <= O_U2
    ALIAS_MB = [("mbT", k_) for k_ in range(KC)]

    def actT_alias(m):
        if m < 32:
            return [("r1", m // 4), ("r1p", m // 4)]
        return ALIAS_MB

    def ffn_evac(m, n, b_gate_or_up, is_gate):
        if is_gate:
            sc.op("act", lambda e: e.activation(out=sgt2[:, n * 512:(n + 1) * 512], in_=ps[b_gate_or_up][:, :], func=AF.Silu),
                  r=[("ps", b_gate_or_up)], w=[("sgt2", n)] + ALIAS_MB)
        else:
            sc.op("dve", lambda e: e.tensor_tensor(out=actT[:, m, n * 512:(n + 1) * 512], in0=ps[b_gate_or_up][:, :],
                                                   in1=sgt2[:, n * 512:(n + 1) * 512], op=ALU.mult),
                  r=[("ps", b_gate_or_up), ("sgt2", n)], w=[("actT", m)] + actT_alias(m))

    def ffn_half(s_w, m, mm, n, par):
        wv_ = wview(s_w, 512)
        for (c0, boff) in ((0, 0), (256, 2)):
            b_ = par + boff + n
            for k in range(KC):
                sc.op("pe", lambda e, b_=b_, k=k, c0=c0, wv_=wv_: e.matmul(
                    ps[b_][:, :], wv_[:, k, c0 + mm * 128:c0 + (mm + 1) * 128], u2T[:, k, n * 512:(n + 1) * 512],
                    start=(k == 0), stop=(k == KC - 1)), r=[("w", s_w), ("u2T", n)], w=[("ps", b_)])
            ffn_evac(m, n, b_, boff == 0)

    def ffn_chunk(s_w, m, mm, par):
        wv_ = wview(s_w, 512)
        for (c0, boff) in ((0, 0), (256, 2)):
            for k in range(KC):
                for n in range(2):
                    sc.op("pe", lambda e, n=n, k=k, c0=c0, boff=boff, wv_=wv_: e.matmul(
                        ps[par + boff + n][:, :], wv_[:, k, c0 + mm * 128:c0 + (mm + 1) * 128], u2T[:, k, n * 512:(n + 1) * 512],
                        start=(k == 0), stop=(k == KC - 1)), r=[("w", s_w), ("u2T", n)], w=[("ps", par + boff + n)])
            for n in range(2):
                ffn_evac(m, n, par + boff + n, boff == 0)

    for tt in range(8):
        rb = r1buf[:, tt, :]
        sc.op("act", lambda e, rb=rb, tt=tt: e.activation(out=rb, in_=rb, func=AF.Identity, bias=nm1[:, tt:tt + 1], scale=rs1[:, tt:tt + 1]),
              r=[("r1", tt), ("rs1",), ("nm1",)], w=[("r1", tt)])
        for q in range(4):
            bk = (tt % 2) * 4 + q
            for j in range(4):
                m = q * 4 + j
                sc.op("pe", lambda e, bk=bk, j=j, m=m, rb=rb: e.transpose(
                    ps[bk][:, j * 128:(j + 1) * 128], rb[:, m * 128:(m + 1) * 128], ident),
                    r=[("r1", tt), ("r1p", tt), CONST], w=[("ps", bk)])
            for j in range(4):
                m = q * 4 + j
                dst = u2T[:, m, tt * 128:(tt + 1) * 128]
                src = ps[bk][:, j * 128:(j + 1) * 128]
                if q % 2 == 0:
                    sc.op("act", lambda e, dst=dst, src=src, m=m: e.activation(
                        out=dst, in_=src, func=AF.Identity, bias=B2c[:, m:m + 1], scale=G2c[:, m:m + 1]),
                        r=[("ps", bk), ("G2c",), ("B2c",)], pw=[("u2T", tt // 4)])
                else:
                    sc.op("dve", lambda e, dst=dst, src=src, m=m: e.tensor_scalar(
                        out=dst, in0=src, scalar1=G2c[:, m:m + 1], scalar2=B2c[:, m:m + 1],
                        op0=ALU.mult, op1=ALU.add), r=[("ps", bk), ("G2c",), ("B2c",)], pw=[("u2T", tt // 4)])
        sc.op("pool", lambda e, rb=rb: e.tensor_tensor(out=rb, in0=rb, in1=gbc, op=ALU.mult),
              r=[("r1", tt), ("gbc",)], w=[("r1p", tt)])
        sc.op("pool", lambda e, rb=rb: e.tensor_tensor(out=rb, in0=rb, in1=bbc, op=ALU.add),
              r=[("r1p", tt), ("bbc",)], w=[("r1p", tt)])
        sc.dma("pool", x1d[tt * 128:(tt + 1) * 128, :], rb, r=[("r1", tt), ("r1p", tt)], w=[("x1d", tt)], sem=("x1st", tt))
        if tt >= 4:
            idx = tt - 4
            ffn_half(pre_ffn[idx // 2], idx, idx % 2, 0, ((tt + 1) % 2) * 4)

    if stop_here("p6"):
        add_dump("u2T", u2T, [128, KC, 1024], BF16)
        return finish(nc, sc, es, dump_out)

    for idx in range(4):
        ffn_half(pre_ffn[idx // 2], idx, idx % 2, 1, (idx % 2) * 4)
    for tp in range(2, NF // 2):
        s_w = wload(wtile_pair(w_ffn_in[:, tp * 256:(tp + 1) * 256], w_ffn_in[:, DFF + tp * 256:DFF + (tp + 1) * 256]))
        for mm in range(2):
            m = tp * 2 + mm
            ffn_chunk(s_w, m, mm, (m % 2) * 4)

    HK = NF // 2

    def fo_parts(mp, k0):
        def dst(sl):
            return sl[:, 0:HK * 256].rearrange("p (k n) -> p k n", k=HK, n=256)
        cols = w_ffn_out[:, mp * 256:(mp + 1) * 256]
        return [(dst, cols[k0 * 128:(k0 + HK) * 128, :].rearrange("(k p) n -> p k n", p=128))]
    n0 = wstate["n"]
    slotT = (n0 + 15) % NSLOT
    slotZ = (n0 + 16) % NSLOT
    pre_fo = (wload(fo_parts(0, 0)), wload(fo_parts(0, HK)))
    sc.barrier()
    O_R2 = ARENA - 65536
    assert O_G + 1536 + 64 + 96 <= O_R2
    r2buf = view(O_R2, [128, 8, 2048])
    stats2 = view(O_G, [128, 8, 8, 6])
    mv2 = view(O_G + 1536, [128, 8, 2])
    sd2 = view(O_G + 1600, [128, 8])
    rs2 = view(O_G + 1632, [128, 8])
    nm2 = view(O_G + 1664, [128, 8])
    fobuf = wslot[slotT][:, HK * 256:HK * 256 + 2048].rearrange("p (a b) -> p a b", a=2, b=1024)
    gb2 = view(4 * KB + slotZ * 16 * KB, [128, 4096])
    gbc2, bbc2 = gb2[:, 0:2048], gb2[:, 2048:4096]
    for tt in range(8):
        sc.dma("sp", r2buf[:, tt, :], x1d[tt * 128:(tt + 1) * 128, :], w=[("r2ld",)], sem=("r2ld",), nodep=(tt > 0))
    R2LD = ("r2ld",)

    def fo_transposes(mp):
        for tt in range(8):
            bk = 6 + tt // 4
            trv = ps[bk][:, :].bitcast(BF16)
            for mo in range(2):
                c0 = (tt % 4) * 256 + mo * 128
                sc.op("pe", lambda e, trv=trv, c0=c0, mo=mo, tt=tt: e.transpose(
                    trv[:, c0:c0 + 128], fobuf[:, mo, tt * 128:(tt + 1) * 128], ident_bf),
                    r=[("fobuf", mo), ("identbf",)], w=[("ps", bk)])
        for tt in range(8):
            bk = 6 + tt // 4
            trv = ps[bk][:, :].bitcast(BF16)
            c0 = (tt % 4) * 256
            dstv = r2buf[:, tt, mp * 256:(mp + 1) * 256]
            sc.op("dve", lambda e, trv=trv, c0=c0, dstv=dstv: e.scalar_tensor_tensor(
                out=dstv, in0=dstv, scalar=ALPHA, in1=trv[:, c0:c0 + 256], op0=ALU.mult, op1=ALU.add),
                r=[("ps", bk), R2LD], pw=[("r2", tt)])
            sc.op("dve", lambda e, dstv=dstv, tt=tt, mp=mp: e.bn_stats(out=stats2[:, tt, mp, :], in_=dstv),
                  r=[("r2", tt)], pw=[("st2", tt)])

    pending_tr = None
    for mp in range(KC // 2):
        if mp == 0:
            s_a, s_b = pre_fo
        else:
            s_a = wload(fo_parts(mp, 0))
            s_b = wload(fo_parts(mp, HK))
        wa_ = wview(s_a, 256, HK)
        wb_ = wview(s_b, 256, HK)

        def bank(mo, n, mp=mp):
            return (4 * mp + 2 * mo + n) % 6
        for ti, (wv2, skey, k0) in enumerate(((wa_, ("w", s_a), 0), (wb_, ("w", s_b), HK))):
            for mo in range(2):
                for kk in range(HK):
                    k = k0 + kk
                    for n in range(2):
                        b_ = bank(mo, n)
                        sc.op("pe", lambda e, b_=b_, mo=mo, n=n, k=k, kk=kk, wv2=wv2: e.matmul(
                            ps[b_][:, :], wv2[:, kk, mo * 128:(mo + 1) * 128], actT[:, k, n * 512:(n + 1) * 512],
                            start=(k == 0), stop=(k == NF - 1)), r=[skey, ("actT", k)], w=[("ps", b_)])
            if ti == 0 and pending_tr is not None:
                fo_transposes(pending_tr)
                pending_tr = None
        for mo in range(2):
            m = mp * 2 + mo
            for n in range(2):
                b_ = bank(mo, n)
                if n == 0:
                    sc.op("act", lambda e, b_=b_, n=n, m=m, mo=mo: e.activation(
                        out=fobuf[:, mo, n * 512:(n + 1) * 512], in_=ps[b_][:, :], func=AF.Identity, scale=modT[:, 80 + m:81 + m]),
                        r=[("ps", b_)] + MODALL, pw=[("fobuf", mo)])
                else:
                    sc.op("dve", lambda e, b_=b_, n=n, m=m, mo=mo: e.tensor_scalar(
                        out=fobuf[:, mo, n * 512:(n + 1) * 512], in0=ps[b_][:, :], scalar1=modT[:, 80 + m:81 + m],
                        scalar2=None, op0=ALU.mult), r=[("ps", b_)] + MODALL, pw=[("fobuf", mo)])
        pending_tr = mp
    sc.dma("sp", gbc2, ln2_g.to_broadcast([128, D]), w=[("w", slotZ)], sem=("w", slotZ))
    sc.dma("sp", bbc2, ln2_b.to_broadcast([128, D]), w=[("w", slotZ)], sem=("w", slotZ), nodep=True)
    fo_transposes(pending_tr)
    for tt in range(8):
        sc.op("dve", lambda e, tt=tt: e.bn_aggr(out=mv2[:, tt, :], in_=stats2[:, tt, :, :].rearrange("p a b -> p (a b)")),
              r=[("st2", tt)], w=[("mv2",)])
    sc.op("dve", lambda e: e.tensor_scalar(out=sd2, in0=mv2[:, :, 1], scalar1=LN_EPS, scalar2=None, op0=ALU.add),
          r=[("mv2",)], w=[("sd2",)])
    sc.op("act", lambda e: e.activation(out=sd2, in_=sd2, func=AF.Sqrt), r=[("sd2",)], w=[("sd2",)])
    sc.op("dve", lambda e: e.reciprocal(out=rs2, in_=sd2), r=[("sd2",)], w=[("rs2",)])
    sc.op("dve", lambda e: e.scalar_tensor_tensor(out=nm2, in0=mv2[:, :, 0], scalar=-1.0, in1=rs2, op0=ALU.mult, op1=ALU.mult),
          r=[("mv2",), ("rs2",)], w=[("nm2",)])
    for q in range(1, 4):
        sc.op("act", lambda e, q=q: e.activation(out=ps[q][:, :], in_=gbc2[:, q * 512:(q + 1) * 512], func=AF.Copy),
              r=[("w", slotZ)], w=[("ps", q)])
        sc.op("act", lambda e, q=q: e.activation(out=ps[4 + q][:, :], in_=bbc2[:, q * 512:(q + 1) * 512], func=AF.Copy),
              r=[("w", slotZ)], w=[("ps", 4 + q)])
    PC = 512
    for tt in range(8):
        rb = r2buf[:, tt, :]
        sc.op("act", lambda e, rb=rb, tt=tt: e.activation(out=rb, in_=rb, func=AF.Identity, bias=nm2[:, tt:tt + 1], scale=rs2[:, tt:tt + 1]),
              r=[("r2", tt), ("rs2",), ("nm2",)], w=[("r2", tt)])
        sc.op("pool", lambda e, rb=rb: e.tensor_tensor(out=rb[:, 0:PC], in0=rb[:, 0:PC], in1=gbc2[:, 0:PC], op=ALU.mult),
              r=[("r2", tt), ("w", slotZ)], w=[("r2p", tt)])
        sc.op("pool", lambda e, rb=rb: e.tensor_tensor(out=rb[:, 0:PC], in0=rb[:, 0:PC], in1=bbc2[:, 0:PC], op=ALU.add),
              r=[("r2p", tt), ("w", slotZ)], w=[("r2p", tt)])
        for q in range(1, 4):
            cs = slice(q * 512, (q + 1) * 512)
            sc.op("dve", lambda e, rb=rb, cs=cs, q=q: e.tensor_tensor(out=rb[:, cs], in0=rb[:, cs], in1=ps[q][:, :], op=ALU.mult),
                  r=[("r2", tt), ("ps", q)], w=[("r2d", tt, q)])
            sc.op("dve", lambda e, rb=rb, cs=cs, q=q: e.tensor_tensor(out=rb[:, cs], in0=rb[:, cs], in1=ps[4 + q][:, :], op=ALU.add),
                  r=[("r2d", tt, q), ("ps", 4 + q)], w=[("r2d", tt, q)])
        sc.dma("pool", y[tt * 128:(tt + 1) * 128, :], rb,
               r=[("r2", tt), ("r2p", tt)] + [("r2d", tt, q) for q in range(1, 4)], w=[("y", tt)], sem=("yst", tt % 3))
    return finish(nc, sc, es, dump_out)


def finish(nc, sc, es, dump_out):
    sc.barrier()
    sc.emit(nc, es)
    es.close()
    return nc, dump_out


def _own_blocks(parity):
    return [2 * j + parity for j in range(8)]


def prep_inputs(inp):
    f = np.float32
    x = np.asarray(inp["x"], f)
    c = np.asarray(inp["c"], f)
    pos = np.asarray(inp["positions"], np.int32)
    w_in = np.ascontiguousarray(np.asarray(inp["w_in"], f)[0])
    w_q_b = np.asarray(inp["w_q_b"], f)[0]
    w_kv_b = np.asarray(inp["w_kv_b"], f)[0]
    rope = w_in[:, 1024:1088]
    swap = np.concatenate([rope[:, 32:64], rope[:, 0:32]], axis=1)
    w_rope = np.ascontiguousarray(np.concatenate([rope, rope, swap, swap], axis=1))
    q3 = w_q_b.reshape(512, NH, 192)
    q_nope = q3[:, :, 0:128].reshape(512, NH * 128)
    q_rope = q3[:, :, 128:192]
    q_swap = np.concatenate([q_rope[:, :, 32:64], q_rope[:, :, 0:32]], axis=2)
    w_qb = np.ascontiguousarray(np.concatenate([q_nope, q_rope.reshape(512, NH * 64), q_swap.reshape(512, NH * 64)], axis=1))
    kv3 = w_kv_b.reshape(512, NH, 256)
    w_kvb = np.ascontiguousarray(np.concatenate([kv3[:, :, 0:128].reshape(512, NH * 128),
                                                 kv3[:, :, 128:256].reshape(512, NH * 128)], axis=1))

    def col(v, n):
        return np.ascontiguousarray(np.asarray(v, f).reshape(n, 128).T)

    shared = {
        "w_ada": np.ascontiguousarray(np.asarray(inp["w_ada"], f)[0]),
        "b_ada_col": col(inp["b_ada"][0], 96),
        "w_in": w_in,
        "w_rope": w_rope,
        "w_qb": w_qb,
        "w_kvb": w_kvb,
        "w_o_a": np.ascontiguousarray(np.asarray(inp["w_o_a"], f)[0]),
        "w_o_b": np.ascontiguousarray(np.asarray(inp["w_o_b"], f)[0]),
        "w_o": np.ascontiguousarray(np.asarray(inp["w_o"], f)[0]),
        "w_ffn_in": np.ascontiguousarray(np.asarray(inp["w_ffn_in"], f)[0]),
        "w_ffn_out": np.ascontiguousarray(np.asarray(inp["w_ffn_out"], f)[0]),
        "gq_col": col(inp["g_q_a"][0], 4),
        "gkv_col": col(inp["g_kv_a"][0], 4),
        "wconv_col": np.ascontiguousarray(np.asarray(inp["w_conv"], f)[0].reshape(3, KC, 128).transpose(2, 1, 0).reshape(128, KC * 3)),
        "ln1g_col": col(inp["ln1_g"][0], KC),
        "ln1b_col": col(inp["ln1_b"][0], KC),
        "ln1_g": np.asarray(inp["ln1_g"], f).reshape(1, D),
        "ln1_b": np.asarray(inp["ln1_b"], f).reshape(1, D),
        "ln2_g": np.asarray(inp["ln2_g"], f).reshape(1, D),
        "ln2_b": np.asarray(inp["ln2_b"], f).reshape(1, D),
        "ident": np.eye(128, dtype=f),
    }
    kk = np.arange(128)[:, None] // 64
    qq = np.arange(128)[None, :] // 64
    shared["mask_own"] = np.where(kk <= qq, 0.0, NEG).astype(f)
    p = np.arange(128)
    inv_freq = (1.0 / (np.float32(10000.0) ** (np.arange(0, 64, 2, dtype=np.float32) / np.float32(64)))).astype(f)
    shared["invf_col"] = inv_freq[p % 32].reshape(128, 1).astype(f)
    shared["sgn_col"] = np.where((p % 64) < 32, -1.0, 1.0).reshape(128, 1).astype(f)

    in_maps = []
    for core in range(8):
        b, par = core // 2, core % 2
        own = _own_blocks(par)
        oth = _own_blocks(1 - par)
        order = np.concatenate([np.arange(g * 128, (g + 1) * 128) for g in own + oth])
        xa = np.ascontiguousarray(x[b][order])
        halo = np.zeros((16, D), f)
        hv = np.zeros((1, 16), f)
        for j, g in enumerate(own):
            if g > 0:
                halo[2 * j:2 * j + 2] = x[b][g * 128 - 2:g * 128]
                hv[0, 2 * j:2 * j + 2] = 1.0
        m = dict(shared)
        m["x_all"] = xa
        m["x_halo"] = halo
        m["x_allT"] = np.ascontiguousarray(np.concatenate([xa[:1024], halo, xa[1024:]], axis=0).T)
        m["halo_valid"] = hv
        m["pos"] = np.ascontiguousarray(pos[b][order].reshape(1, S))
        m["c_col"] = col(c[b], KC)
        m["mask_oth"] = np.full((128, 128), 0.0 if par == 1 else NEG, f)
        in_maps.append(m)
    return in_maps


_CACHE = {}


def kernel(**inputs):
    in_maps = prep_inputs(inputs)
    if "nc" not in _CACHE:
        _CACHE["nc"] = build_program(STOP, DUMPS)
    nc, dump_out = _CACHE["nc"]
    res = run_bass_kernel_spmd(nc, in_maps[:NCORES], core_ids=list(range(NCORES)))
    if STOP is not None:
        return res
    out = np.zeros((4, S, D), np.float32)
    for core in range(8):
        b, par = core // 2, core % 2
        yc = np.asarray(res.results[core]["y"], np.float32)
        for j, g in enumerate(_own_blocks(par)):
            out[b, g * 128:(g + 1) * 128] = yc[j * 128:(j + 1) * 128]
    return out
```

```python
import math
from contextlib import ExitStack

import numpy as np
import concourse.bass as bass
import concourse.mybir as mybir
from concourse.bass_utils import run_bass_kernel_spmd

F32 = mybir.dt.float32
BF16 = mybir.dt.bfloat16
I32 = mybir.dt.int32
AF = mybir.ActivationFunctionType
ALU = mybir.AluOpType

D = 2048
S = 2048
T = 1024
NH = 16
DFF = 5632
KC = D // 128
ALPHA = (2.0 * 1) ** 0.25
LN_EPS = 1e-5
RMS_EPS = 1e-6
ATTN_SCALE = (128 + 64) ** -0.5
NEG = -1.0e30

SELF_SYNC = True
ENGS = ["pe", "act", "dve", "pool", "sp"]


class _Op:
    __slots__ = ("eng", "fn", "r", "w", "pw", "dma", "dma_cnt", "barrier", "signal", "waits", "nodep")

    def __init__(self, eng, fn, r, w, dma=None, barrier=False, nodep=False, pw=()):
        self.eng = eng
        self.fn = fn
        self.r = tuple(r)
        self.w = tuple(w)
        self.pw = tuple(pw)
        self.dma = dma
        self.dma_cnt = 0
        self.barrier = barrier
        self.signal = False
        self.waits = []
        self.nodep = nodep


class Sched:
    def __init__(self):
        self.ops = []

    def op(self, eng, fn, r=(), w=(), pw=()):
        self.ops.append(_Op(eng, fn, r, w, pw=pw))

    def dma(self, q, out, in_, r=(), w=(), sem=None, nodep=False):
        assert sem is not None
        self.ops.append(_Op(q, lambda e, o=out, i=in_: e.dma_start(out=o, in_=i), r, w, dma=sem, nodep=nodep))

    def barrier(self):
        self.ops.append(_Op("sp", lambda e: e.nop(nofuse=True), (), (), barrier=True))

    def analyze(self):
        ops = self.ops
        last_w = {}
        readers = {}
        pwriters = {}
        eng_last = {}
        dma_count = {}
        fence = None
        known_e = {e: {e2: -1 for e2 in ENGS} for e in ENGS}
        known_d = {e: {} for e in ENGS}
        for i, op in enumerate(ops):
            deps = set()
            dwait = {}
            if op.barrier:
                deps.update(eng_last.values())
                for k, c in dma_count.items():
                    dwait[k] = c
            elif not op.nodep:
                for k in op.r:
                    j = last_w.get(k)
                    if j is not None:
                        deps.add(j)
                    pwk = pwriters.get(k)
                    if pwk:
                        deps.update(pwk.values())
                for k in op.w:
                    j = last_w.get(k)
                    if j is not None:
                        deps.add(j)
                    rd = readers.get(k)
                    if rd:
                        deps.update(rd.values())
                    pwk = pwriters.get(k)
                    if pwk:
                        deps.update(pwk.values())
                for k in op.pw:
                    j = last_w.get(k)
                    if j is not None:
                        deps.add(j)
                    rd = readers.get(k)
                    if rd:
                        deps.update(rd.values())
                if fence is not None:
                    deps.add(fence)
            deps.discard(i)
            emax = {}
            for j in deps:
                oj = ops[j]
                if oj.dma is not None:
                    if dwait.get(oj.dma, 0) < oj.dma_cnt:
                        dwait[oj.dma] = oj.dma_cnt
                else:
                    if oj.eng == op.eng and (op.eng == "pe" or not SELF_SYNC) and op.dma is None:
                        continue
                    if emax.get(oj.eng, -1) < j:
                        emax[oj.eng] = j
            x = op.eng
            for e2, j in emax.items():
                if known_e[x][e2] >= j:
                    continue
                known_e[x][e2] = j
                ops[j].signal = True
                op.waits.append(("e", e2, j))
            for k, c in dwait.items():
                if known_d[x].get(k, 0) >= c:
                    continue
                known_d[x][k] = c
                op.waits.append(("d", k, c))
            if op.dma is not None:
                dma_count[op.dma] = dma_count.get(op.dma, 0) + 1
                op.dma_cnt = dma_count[op.dma]
            else:
                eng_last[op.eng] = i
                known_e[x][x] = max(known_e[x][x], -1)
            if op.barrier:
                op.signal = True
                last_w.clear()
                readers.clear()
                pwriters.clear()
                fence = i
            else:
                for k in op.r:
                    rd = readers.setdefault(k, {})
                    if op.dma is not None:
                        rd[("dma", i)] = i
                    else:
                        rd[op.eng] = i
                for k in op.w:
                    last_w[k] = i
                    readers[k] = {}
                    pwriters[k] = {}
                for k in op.pw:
                    pwriters.setdefault(k, {})[op.eng if op.dma is None else ("dma", i)] = i
        cnt = {e: 0 for e in ENGS}
        self.sigval = {}
        for i, op in enumerate(ops):
            if op.dma is None and op.signal:
                cnt[op.eng] += 1
                self.sigval[i] = cnt[op.eng]
        self.dma_keys = sorted({op.dma for op in ops if op.dma is not None}, key=str)

    def emit(self, nc, es):
        self.analyze()
        esem = {e: es.enter_context(nc.semaphore("e_" + e)) for e in ENGS}
        dsem = {k: es.enter_context(nc.semaphore("d%d" % i)) for i, k in enumerate(self.dma_keys)}
        per = {e: [] for e in ENGS}
        for i, op in enumerate(self.ops):
            per[op.eng].append((i, op))
        sigval = self.sigval

        def run(name, eng):
            for i, op in per[name]:
                for kind, key, val in op.waits:
                    if kind == "e":
                        eng.wait_ge(esem[key], sigval[val])
                    else:
                        eng.wait_ge(dsem[key], 16 * val)
                ins = op.fn(eng)
                if op.dma is not None:
                    ins.then_inc(dsem[op.dma], 16)
                elif op.signal:
                    ins.then_inc(esem[name], 1)

        block = es.enter_context(nc.Block())
        block.tensor(lambda e: run("pe", e))
        block.scalar(lambda e: run("act", e))
        block.vector(lambda e: run("dve", e))
        block.gpsimd(lambda e: run("pool", e))
        block.sync(lambda e: run("sp", e))


STOP = None
NCORES = 8
DUMPS = ()


def build_program(stop=None, dumps=()):
    nc = bass.Bass("TRN2", target_bir_lowering=False)
    sc = Sched()
    es = ExitStack()

    def din(name, shape, dt=F32):
        return nc.dram_tensor(name, list(shape), dt, kind="ExternalInput").ap()

    x_all = din("x_all", [S, D])
    x_halo = din("x_halo", [16, D])
    x_allT = din("x_allT", [D, 2064])
    pos_i = din("pos", [1, S], I32)
    halo_valid = din("halo_valid", [1, 16])
    c_col = din("c_col", [128, KC])
    ident_d = din("ident", [128, 128])
    mask_own_d = din("mask_own", [128, 128])
    mask_oth_d = din("mask_oth", [128, 128])
    invf_d = din("invf_col", [128, 1])
    sgn_d = din("sgn_col", [128, 1])
    w_ada = din("w_ada", [D, 6 * D])
    b_ada_col = din("b_ada_col", [128, 96])
    w_in = din("w_in", [D, 11328])
    w_rope = din("w_rope", [D, 256])
    w_qb = din("w_qb", [512, 4096])
    w_kvb = din("w_kvb", [512, 4096])
    w_o_a = din("w_o_a", [D, D])
    w_o_b = din("w_o_b", [D, D])
    w_o = din("w_o", [D, D])
    w_ffn_in = din("w_ffn_in", [D, 2 * DFF])
    w_ffn_out = din("w_ffn_out", [DFF, D])
    gq_col = din("gq_col", [128, 4])
    gkv_col = din("gkv_col", [128, 4])
    wconv_col = din("wconv_col", [128, KC * 3])
    ln1g_col = din("ln1g_col", [128, KC])
    ln1b_col = din("ln1b_col", [128, KC])
    ln1_g = din("ln1_g", [1, D])
    ln1_b = din("ln1_b", [1, D])
    ln2_g = din("ln2_g", [1, D])
    ln2_b = din("ln2_b", [1, D])
    y = nc.dram_tensor("y", [T, D], F32, kind="ExternalOutput").ap()
    x1d = nc.dram_tensor("x1d", [T, D], F32, kind="Internal").ap()

    dump_out = {}

    ARENA = 212480
    arena = es.enter_context(nc.sbuf_tensor("arena", [128, ARENA // 4], F32))

    def view(off, shape, dt=F32):
        esz = 4 if dt in (F32, I32) else 2
        n = 1
        for s_ in shape[1:]:
            n *= s_
        nb = n * esz
        assert off % 4 == 0 and nb % 4 == 0 and off + nb <= ARENA, (off, shape)
        a = arena[:, off // 4:(off + nb) // 4]
        if dt != F32:
            a = a.bitcast(dt)
        if len(shape) == 3:
            a = a.rearrange("p (a b) -> p a b", a=shape[1], b=shape[2])
        elif len(shape) == 4:
            a = a.rearrange("p (a b c) -> p a b c", a=shape[1], b=shape[2], c=shape[3])
        return a

    ps = [es.enter_context(nc.psum_tensor("ps%d" % i, [128, 512], F32)) for i in range(8)]

    KB = 1024
    o = 0
    ident = view(o, [128, 128]); o += 512
    ident_bf = view(o, [128, 128], BF16); o += 256
    mask_own = view(o, [128, 128], BF16); o += 256
    mask_oth = view(o, [128, 128], BF16); o += 256
    ones_bf = view(o, [128, 128], BF16); o += 256
    modT = view(o, [128, 96]); o += 384
    sc1p = view(o, [128, 16]); o += 64
    sc2p = view(o, [128, 16]); o += 64
    ccol = view(o, [128, 16]); o += 64
    cact = view(o, [128, 16], BF16); o += 32
    badac = view(o, [128, 96]); o += 384
    gq = view(o, [128, 4]); o += 16
    gkv = view(o, [128, 4]); o += 16
    wconv = view(o, [128, 48]); o += 192
    hvalid = view(o, [128, 16]); o += 64
    invf = view(o, [128, 1]); o += 4
    sgn = view(o, [128, 1]); o += 4
    small = view(o, [128, 64]); o += 256
    g1c = view(o, [128, 16]); o += 64
    b1c = view(o, [128, 16]); o += 64
    G2c = view(o, [128, 16]); o += 64
    B2c = view(o, [128, 16]); o += 64
    assert o <= 4 * KB
    NSLOT = 3
    wslot = [view(4 * KB + i * 16 * KB, [128, 8192], BF16) for i in range(NSLOT)]
    wstate = {"n": 0}

    def wload(parts):
        s_ = wstate["n"] % NSLOT
        wstate["n"] += 1
        for pi, (dst_fn, src) in enumerate(parts):
            sc.dma("pool", dst_fn(wslot[s_]), src, w=[("w", s_)], sem=("w", s_), nodep=(pi > 0))
        return s_

    def wtile_std(src_cols_ap, ncols, kchunks=KC):
        def dst(sl):
            return sl[:, 0:kchunks * ncols].rearrange("p (k n) -> p k n", k=kchunks, n=ncols)
        return (dst, src_cols_ap.rearrange("(k p) n -> p k n", p=128))

    def wtile_pair(src_a, src_b):
        def dst_a(sl):
            return sl[:, 0:KC * 512].rearrange("p (k n) -> p k n", k=KC, n=512)[:, :, 0:256]

        def dst_b(sl):
            return sl[:, 0:KC * 512].rearrange("p (k n) -> p k n", k=KC, n=512)[:, :, 256:512]
        return [(dst_a, src_a.rearrange("(k p) n -> p k n", p=128)), (dst_b, src_b.rearrange("(k p) n -> p k n", p=128))]

    def wview(s_, ncols, kchunks=KC, off=0):
        return wslot[s_][:, off:off + kchunks * ncols].rearrange("p (k n) -> p k n", k=kchunks, n=ncols)

    B0 = 52 * KB
    uT_own = view(B0, [128, KC, 1040], BF16)
    O_UOTH = B0 + 33280
    uT_oth = view(O_UOTH, [128, KC, 1024], BF16)
    O_XS = O_UOTH + 32768
    xs = [view(O_XS + i * 8192, [128, 2048]) for i in range(2)]
    O_TAB = O_XS + 16384
    cosT = view(O_TAB, [128, 2048])
    sinT = view(O_TAB + 8192, [128, 2048])
    O_KVN = O_TAB + 16384
    kvnT = view(O_KVN, [128, 4, 2048], BF16)
    O_QN = O_KVN + 16384
    qnT = view(O_QN, [128, 4, 1024], BF16)
    O_KR = O_QN + 8192
    krT = view(O_KR, [128, 2048], BF16)
    O_TMP = O_KR + 4096
    assert O_TMP + 24 * KB <= ARENA

    def stop_here(name):
        return stop == name

    def add_dump(name, ap, shape, dt):
        t = nc.dram_tensor("dbg_" + name, list(shape), dt, kind="ExternalOutput").ap()
        dump_out[name] = t
        sc.barrier()
        sc.dma("sp", t, ap, sem=("dump", name))

    tmpc = view(O_TMP, [128, 128])
    tmpc2 = view(O_TMP + 512, [128, 128])
    for dst, src in [(ident, ident_d), (tmpc, mask_own_d), (tmpc2, mask_oth_d), (ccol, c_col),
                     (badac, b_ada_col), (gq, gq_col), (gkv, gkv_col), (wconv, wconv_col),
                     (invf, invf_d), (sgn, sgn_d), (g1c, ln1g_col), (b1c, ln1b_col),
                     (hvalid, halo_valid.to_broadcast([128, 16]))]:
        sc.dma("sp", dst, src, w=[("const",)], sem=("const",), nodep=True)
    CONST = ("const",)
    sc.op("dve", lambda e: e.tensor_copy(out=ident_bf, in_=ident), r=[CONST], w=[("identbf",)])
    sc.op("dve", lambda e: e.tensor_copy(out=mask_own, in_=tmpc), r=[CONST], w=[("masks",)])
    sc.op("dve", lambda e: e.tensor_copy(out=mask_oth, in_=tmpc2), r=[CONST], w=[("masks",)])
    sc.op("dve", lambda e: e.memset(ones_bf, 1.0), w=[("ones",)])
    sc.op("act", lambda e: e.activation(out=cact, in_=ccol, func=AF.Silu), r=[CONST], w=[("cact",)])

    def adaln(tiles, bank):
        for tj in tiles:
            s_ = wload([wtile_std(w_ada[:, tj * 512:(tj + 1) * 512], 512)])
            wv = wview(s_, 512)
            for mm in range(4):
                m = tj * 4 + mm
                for k in range(KC):
                    sc.op("pe", lambda e, m=m, k=k, mm=mm, wv=wv: e.matmul(
                        ps[bank][:, m:m + 1], wv[:, k, mm * 128:(mm + 1) * 128], cact[:, k:k + 1],
                        start=(k == 0), stop=(k == KC - 1)),
                        r=[("w", s_), ("cact",)], w=[("ps", bank)])
        lo, hi = tiles[0] * 4, tiles[-1] * 4 + 4
        sc.op("dve", lambda e: e.tensor_tensor(out=modT[:, lo:hi], in0=ps[bank][:, lo:hi], in1=badac[:, lo:hi],
                                               op=ALU.add), r=[("ps", bank), CONST], w=[("modT", lo)])

    posb = view(O_TMP, [128, 2048], I32)
    tA = view(O_TMP + 8192, [128, 2048])
    tB = view(O_TMP + 16384, [128, 2048])
    sc.dma("sp", posb, pos_i.to_broadcast([128, S]), r=[("masks",)], w=[("posb",)], sem=("posb",))
    sc.op("dve", lambda e: e.tensor_copy(out=tA, in_=posb), r=[("posb",)], w=[("tA",)])
    sc.op("dve", lambda e: e.tensor_scalar(out=sinT, in0=tA, scalar1=invf[:, 0:1], scalar2=None, op0=ALU.mult),
          r=[("tA",), CONST], w=[("ang",)])
    ang = sinT
    TWO_PI = 2.0 * math.pi
    C1 = 6.28125
    C2 = TWO_PI - C1
    ki = posb

    def table(dst, shift, scale_ap):
        sc.op("dve", lambda e: e.tensor_scalar(out=tA, in0=ang, scalar1=1.0 / TWO_PI, scalar2=shift / TWO_PI + 0.5,
                                               op0=ALU.mult, op1=ALU.add), r=[("ang",)], w=[("tA",)])
        sc.op("dve", lambda e: e.tensor_copy(out=ki, in_=tA), r=[("tA",)], w=[("ki",)])
        sc.op("dve", lambda e: e.tensor_copy(out=tA, in_=ki), r=[("ki",)], w=[("tA",)])
        sc.op("dve", lambda e: e.tensor_scalar(out=tB, in0=ang, scalar1=shift, scalar2=None, op0=ALU.add),
              r=[("ang",)], w=[("tB",)])
        sc.op("dve", lambda e: e.scalar_tensor_tensor(out=tB, in0=tA, scalar=-C1, in1=tB, op0=ALU.mult, op1=ALU.add),
              r=[("tA",), ("tB",)], w=[("tB",)])
        sc.op("dve", lambda e: e.scalar_tensor_tensor(out=tB, in0=tA, scalar=-C2, in1=tB, op0=ALU.mult, op1=ALU.add),
              r=[("tA",), ("tB",)], w=[("tB",)])
        sc.op("dve", lambda e: e.tensor_scalar(out=tA, in0=tB, scalar1=-math.pi, scalar2=TWO_PI, op0=ALU.is_lt, op1=ALU.mult),
              r=[("tB",)], w=[("tA",)])
        sc.op("dve", lambda e: e.tensor_tensor(out=tB, in0=tB, in1=tA, op=ALU.add), r=[("tA",), ("tB",)], w=[("tB",)])
        sc.op("dve", lambda e: e.tensor_scalar(out=tA, in0=tB, scalar1=math.pi, scalar2=-TWO_PI, op0=ALU.is_gt, op1=ALU.mult),
              r=[("tB",)], w=[("tA",)])
        sc.op("dve", lambda e: e.tensor_tensor(out=tB, in0=tB, in1=tA, op=ALU.add), r=[("tA",), ("tB",)], w=[("tB",)])
        sc.op("dve", lambda e: e.tensor_scalar(out=tB, in0=tB, scalar1=-math.pi, scalar2=math.pi, op0=ALU.max, op1=ALU.min),
              r=[("tB",)], w=[("tB",)])
        if scale_ap is None:
            sc.op("act", lambda e: e.activation(out=dst, in_=tB, func=AF.Sin), r=[("tB",)], w=[("tab", id(dst))])
        else:
            sc.op("act", lambda e: e.activation(out=dst, in_=tB, func=AF.Sin, scale=scale_ap),
                  r=[("tB",), CONST], w=[("tab", id(dst))])

    table(cosT, math.pi / 2.0, None)
    table(sinT, 0.0, sgn[:, 0:1])
    TABS = [("tab", id(cosT)), ("tab", id(sinT))]

    adaln(list(range(0, 8)), 0)
    sc.op("dve", lambda e: e.tensor_scalar(out=sc1p, in0=modT[:, 16:32], scalar1=1.0, scalar2=None, op0=ALU.add),
          r=[("modT", 0)], w=[("mod1",)])
    MOD1 = ("mod1",)

    stgA = [view(O_XS + i * 4160, [128, 1040]) for i in range(2)] + [view(O_KVN + i * 4160, [128, 1040]) for i in range(6)]
    assert O_KVN + 6 * 4160 <= O_TMP
    stgB = [view(O_XS + 8320, [128, 1024]), view(O_TMP + 24 * KB, [128, 1024])]
    assert O_XS + 8320 + 4096 <= O_TAB and O_TMP + 24 * KB + 4096 <= ARENA

    def u_piece(which, m):
        if which == "own":
            sb = stgA[m % 8]
            sk = ("stgA", m % 8)
            src_d = x_allT[m * 128:(m + 1) * 128, 0:1040]
            dst = uT_own[:, m, 0:1040]
            keys = [("uT", "own", 0), ("uT", "own", 1), ("uT", "halo")]
        else:
            sb = stgB[m % 2]
            sk = ("stgB", m % 2)
            src_d = x_allT[m * 128:(m + 1) * 128, 1040:2064]
            dst = uT_oth[:, m, :]
            keys = [("uT", "oth", 0), ("uT", "oth", 1)]
        sc.dma("sp", sb, src_d, w=[sk], sem=sk)
        if m % 2 == 0:
            sc.op("act", lambda e: e.activation(
                out=dst, in_=sb, func=AF.Identity, bias=modT[:, m:m + 1], scale=sc1p[:, m:m + 1]),
                r=[sk, MOD1, ("modT", 0)], pw=keys)
        else:
            sc.op("dve", lambda e: e.tensor_scalar(
                out=dst, in0=sb, scalar1=sc1p[:, m:m + 1], scalar2=modT[:, m:m + 1],
                op0=ALU.mult, op1=ALU.add),
                r=[sk, MOD1, ("modT", 0)], pw=keys)

    for m in range(KC):
        u_piece("own", m)
    oth_todo = list(range(KC))

    def slip_oth(n=1):
        for _ in range(n):
            if oth_todo:
                u_piece("oth", oth_todo.pop(0))

    if stop_here("p1"):
        add_dump("uT_own", uT_own, [128, KC, 1040], BF16)
        add_dump("uT_oth", uT_oth, [128, KC, 1024], BF16)
        add_dump("modT", modT, [128, 96], F32)
        return finish(nc, sc, es, dump_out)

    s_kv = wload([wtile_std(w_in[:, 512:1024], 512)])
    s_r = wload([wtile_std(w_rope[:, 0:256], 256)])
    s_q = wload([wtile_std(w_in[:, 0:512], 512)])
    wkv, wq, wr = wview(s_kv, 512), wview(s_q, 512), wview(s_r, 256)
    a32 = [view(O_TMP + i * 2048, [128, 512]) for i in range(4)]
    sqb = [view(O_TMP + 8192 + i * 1024, [128, 512], BF16) for i in range(4)]
    rstd = view(O_TMP + 12288, [128, 512])
    rtmp = view(O_TMP + 14336, [128, 512])
    rtmp2 = view(O_TMP + 16384, [128, 512])
    bank_rr = {"n": 0}

    def nbank(pool):
        b = pool[bank_rr["n"] % len(pool)]
        bank_rr["n"] += 1
        return b

    def ugrp(g):
        if g < 2:
            return (lambda k: uT_own[:, k, g * 512:(g + 1) * 512]), ("uT", "own", g)
        return (lambda k: uT_oth[:, k, (g - 2) * 512:(g - 1) * 512]), ("uT", "oth", g - 2)

    def latent_norm(g, wv, wkey, gcol, outT, okey, ssq_bank, tag):
        uf, ukey = ugrp(g)
        for m in range(4):
            b = nbank([0, 1, 2, 3, 4, 5])
            for k in range(KC):
                sc.op("pe", lambda e, b=b, m=m, k=k: e.matmul(ps[b][:, :], wv[:, k, m * 128:(m + 1) * 128], uf(k),
                                                              start=(k == 0), stop=(k == KC - 1)),
                      r=[wkey, ukey], w=[("ps", b)])
            sc.op("act", lambda e, b=b, m=m: e.activation(out=a32[m], in_=ps[b][:, :], func=AF.Copy),
                  r=[("ps", b)] + TABS, w=[("a32", m)])
            sc.op("act", lambda e, b=b, m=m: e.activation(out=sqb[m], in_=ps[b][:, :], func=AF.Square),
                  r=[("ps", b)] + TABS, w=[("sqb", m)])
            slip_oth()
        for m in range(4):
            sc.op("pe", lambda e, m=m: e.matmul(ps[ssq_bank][:, :], ones_bf, sqb[m], start=(m == 0), stop=(m == 3)),
                  r=[("sqb", m), ("ones",)], w=[("ps", ssq_bank)])
        sc.op("dve", lambda e: e.tensor_scalar(out=rtmp, in0=ps[ssq_bank][:, :], scalar1=1.0 / 512.0, scalar2=RMS_EPS,
                                               op0=ALU.mult, op1=ALU.add), r=[("ps", ssq_bank)] + TABS, w=[("rtmp",)])
        sc.op("act", lambda e: e.activation(out=rtmp, in_=rtmp, func=AF.Sqrt), r=[("rtmp",)], w=[("rtmp",)])
        sc.op("dve", lambda e: e.reciprocal(out=rstd, in_=rtmp), r=[("rtmp",)], w=[("rstd",)])
        for m in range(4):
            sc.op("dve", lambda e, m=m: e.scalar_tensor_tensor(
                out=outT[:, m, g * 512:(g + 1) * 512], in0=a32[m], scalar=gcol[:, m:m + 1], in1=rstd,
                op0=ALU.mult, op1=ALU.mult), r=[("a32", m), ("rstd",), CONST], w=[okey(g)])

    for g in range(4):
        latent_norm(g, wkv, ("w", s_kv), gkv, kvnT, lambda g_: ("kvnT", g_), 6, "kv")
        uf, ukey = ugrp(g)
        bA = nbank([0, 1, 2, 3, 4, 5])
        bB = nbank([0, 1, 2, 3, 4, 5])
        for (b, c0) in ((bA, 0), (bB, 128)):
            for k in range(KC):
                sc.op("pe", lambda e, b=b, c0=c0, k=k, uf=uf: e.matmul(ps[b][:, :], wr[:, k, c0:c0 + 128], uf(k),
                                                                start=(k == 0), stop=(k == KC - 1)),
                      r=[("w", s_r), ukey], w=[("ps", b)])
        cs = slice(g * 512, (g + 1) * 512)
        sc.op("dve", lambda e, bA=bA, cs=cs: e.tensor_tensor(out=rtmp2, in0=ps[bA][:, :], in1=cosT[:, cs], op=ALU.mult),
              r=[("ps", bA)] + TABS, w=[("rtmp2",)])
        sc.op("dve", lambda e, bB=bB, cs=cs: e.tensor_tensor(out=rtmp, in0=ps[bB][:, :], in1=sinT[:, cs], op=ALU.mult),
              r=[("ps", bB)] + TABS, w=[("rtmp",)])
        sc.op("dve", lambda e, cs=cs: e.tensor_tensor(out=krT[:, cs], in0=rtmp, in1=rtmp2, op=ALU.add),
              r=[("rtmp",), ("rtmp2",)], w=[("krT", g)])
        if g < 2:
            latent_norm(g, wq, ("w", s_q), gq, qnT, lambda g_: ("qnT", g_), 7, "q")
        while g == 1 and oth_todo:
            slip_oth()

    if stop_here("p2"):
        add_dump("kvnT", kvnT, [128, 4, 2048], BF16)
        add_dump("qnT", qnT, [128, 4, 1024], BF16)
        add_dump("krT", krT, [128, 2048], BF16)
        add_dump("cosT", cosT, [128, 2048], F32)
        add_dump("sinT", sinT, [128, 2048], F32)
        return finish(nc, sc, es, dump_out)

    sc.barrier()
    attnT = view(O_UOTH, [128, NH, 1024], BF16)
    O_A = O_XS
    KT = view(O_A, [128, 2, 2048], BF16)
    QT = view(O_A + 8192, [128, 2, 1024], BF16)
    qrT = view(O_A + 12288, [128, 1024], BF16)
    Osb = view(O_A + 14336, [128, 8, 128], BF16)
    O_B = O_TMP
    Vaug = view(O_B, [128, 16, 2, 130], BF16)
    PT = [view(O_B + 8320 + i * 1024, [128, 512], BF16) for i in range(6)]
    qtmp = view(O_B + 8320 + 6144, [128, 512])
    qtmp2 = view(O_B + 8320 + 8192, [128, 512])
    rcp = view(O_B + 8320 + 10240, [128, 8])
    qrTz = [view(O_A + 12288, [128, 1024], BF16), view(O_B + 8320 + 10240 + 64, [128, 1024], BF16)]
    assert O_B + 8320 + 10240 + 64 + 2048 <= ARENA
    sc.op("dve", lambda e: e.memset(qrTz[0][64:128, :], 0.0), w=[("qrT", 0)])
    sc.op("dve", lambda e: e.memset(qrTz[1][0:64, :], 0.0), w=[("qrT", 1)])
    sc.op("dve", lambda e: e.memset(Vaug[:, :, :, 128:130], 1.0), w=[("Vones",)])
    PROJ_BANKS = [3, 4, 5, 6, 7]
    O_BANK = [0, 1, 2]
    ST_BANKS = [3, 4, 5, 6]
    st_n = {"n": 0}
    TR_BANK = 7
    evac_rr = {"n": 0}

    def evac_copy(dst, src, r, w):
        evac_rr["n"] += 1
        if evac_rr["n"] % 2:
            sc.op("act", lambda e: e.activation(out=dst, in_=src, func=AF.Copy), r=r, pw=w)
        else:
            sc.op("dve", lambda e: e.tensor_copy(out=dst, in_=src), r=r, pw=w)

    pt_n = {"n": 0}
    deferred = []
    ada_rest = list(range(8, 24))
    for hp in range(8):
        def part(off, ncols, src):
            def dst(sl):
                return sl[:, off:off + 4 * ncols].rearrange("p (k n) -> p k n", k=4, n=ncols)
            return (dst, src.rearrange("(k p) n -> p k n", p=128))
        s_ = wload([
            part(0, 256, w_kvb[:, hp * 256:(hp + 1) * 256]),
            part(1024, 256, w_kvb[:, 2048 + hp * 256:2048 + (hp + 1) * 256]),
            part(2048, 256, w_qb[:, hp * 256:(hp + 1) * 256]),
            part(3072, 128, w_qb[:, 2048 + hp * 128:2048 + (hp + 1) * 128]),
            part(3584, 128, w_qb[:, 3072 + hp * 128:3072 + (hp + 1) * 128]),
        ])
        wk_ = wview(s_, 256, 4, 0)
        wv_ = wview(s_, 256, 4, 1024)
        wqn_ = wview(s_, 256, 4, 2048)
        wqr_ = wview(s_, 128, 4, 3072)
        wqs_ = wview(s_, 128, 4, 3584)
        WK = ("w", s_)
        for n in range(2):
            bA = nbank(PROJ_BANKS)
            bB = nbank(PROJ_BANKS)
            for (b, wv2) in ((bA, wqr_), (bB, wqs_)):
                for k in range(4):
                    sc.op("pe", lambda e, b=b, n=n, k=k, wv2=wv2: e.matmul(
                        ps[b][:, :], wv2[:, k, :], qnT[:, k, n * 512:(n + 1) * 512],
                        start=(k == 0), stop=(k == 3)), r=[WK, ("qnT", n)], w=[("ps", b)])
            cs = slice(n * 512, (n + 1) * 512)
            sc.op("dve", lambda e, bA=bA, cs=cs: e.tensor_tensor(out=qtmp, in0=ps[bA][:, :], in1=cosT[:, cs], op=ALU.mult),
                  r=[("ps", bA)] + TABS, w=[("qtmp",)])
            sc.op("dve", lambda e, bB=bB, cs=cs: e.tensor_tensor(out=qtmp2, in0=ps[bB][:, :], in1=sinT[:, cs], op=ALU.mult),
                  r=[("ps", bB)] + TABS, w=[("qtmp2",)])
            for hh in range(2):
                rp_ = slice(hh * 64, (hh + 1) * 64)
                sc.op("dve", lambda e, cs=cs, hh=hh, rp_=rp_: e.tensor_tensor(
                    out=qrTz[hh][rp_, cs], in0=qtmp[rp_, :], in1=qtmp2[rp_, :], op=ALU.add),
                    r=[("qtmp",), ("qtmp2",)], w=[("qrT", hh)])
        for h in range(2):
            for n in range(4):
                b = nbank(PROJ_BANKS)
                for k in range(4):
                    sc.op("pe", lambda e, b=b, h=h, n=n, k=k, wk_=wk_: e.matmul(
                        ps[b][:, :], wk_[:, k, h * 128:(h + 1) * 128], kvnT[:, k, n * 512:(n + 1) * 512],
                        start=(k == 0), stop=(k == 3)), r=[WK, ("kvnT", n)], w=[("ps", b)])
                evac_copy(KT[:, h, n * 512:(n + 1) * 512], ps[b][:, :], [("ps", b)], [("KT", h)])
        for tp in range(8):
            b = nbank(PROJ_BANKS)
            for half in range(2):
                tt = tp * 2 + half
                for k in range(4):
                    sc.op("pe", lambda e, b=b, tt=tt, half=half, k=k, wv_=wv_: e.matmul(
                        ps[b][:, half * 256:(half + 1) * 256], kvnT[:, k, tt * 128:(tt + 1) * 128], wv_[:, k, :],
                        start=(k == 0 and half == 0), stop=(k == 3), skip_group_check=True),
                        r=[WK, ("kvnT", tt // 4)], w=[("ps", b)])
            src = ps[b][:, :].rearrange("p (t h d) -> p t h d", t=2, h=2, d=128)
            evac_copy(Vaug[:, tp * 2:tp * 2 + 2, :, 0:128], src, [("ps", b)], [("V",)])
        for h in range(2):
            for n in range(2):
                b = nbank(PROJ_BANKS)
                for k in range(4):
                    sc.op("pe", lambda e, b=b, h=h, n=n, k=k, wqn_=wqn_: e.matmul(
                        ps[b][:, :], wqn_[:, k, h * 128:(h + 1) * 128], qnT[:, k, n * 512:(n + 1) * 512],
                        start=(k == 0), stop=(k == 3)), r=[WK, ("qnT", n)], w=[("ps", b)])
                evac_copy(QT[:, h, n * 512:(n + 1) * 512], ps[b][:, :], [("ps", b)], [("QT", h)])
        for h in range(2):
            hg = hp * 2 + h
            rp = slice(h * 64, (h + 1) * 64)
            units = []
            for c in range(16):
                i = c // 2
                nq = (8 - i) * 128
                for (p0, pw) in [(0, min(512, nq))] + ([(512, nq - 512)] if nq > 512 else []):
                    units.append((c, i, p0, pw))

            def emit_st(u, uinfo):
                c, i, p0, pw = uinfo
                own = (c % 2 == 0)
                tt = i if own else 8 + i
                kcs = slice(tt * 128, (tt + 1) * 128)
                q0 = i * 128
                b = ST_BANKS[st_n["n"] % len(ST_BANKS)]
                st_n["n"] += 1
                slot = pt_n["n"] % len(PT)
                pt_n["n"] += 1
                qs = slice(q0 + p0, q0 + p0 + pw)
                sc.op("pe", lambda e, b=b, pw=pw, kcs=kcs, qs=qs, h=h: e.matmul(
                    ps[b][:, 0:pw], KT[:, h, kcs], QT[:, h, qs], start=True, stop=False),
                    r=[("KT", h), ("QT", h)], w=[("ps", b)])
                sc.op("pe", lambda e, b=b, pw=pw, kcs=kcs, qs=qs, p0=p0, h=h: e.matmul(
                    ps[b][:, 0:pw], krT[:, kcs], qrTz[h][:, qs], start=False, stop=(p0 != 0)),
                    r=[("krT", tt // 4), ("qrT", h)], w=[("ps", b)])
                if p0 == 0:
                    mk = mask_own if own else mask_oth
                    sc.op("pe", lambda e, b=b, mk=mk: e.matmul(
                        ps[b][:, 0:128], ident_bf, mk, start=False, stop=True),
                        r=[("identbf",), ("masks",)], w=[("ps", b)])
                sc.op("act", lambda e, b=b, pw=pw, slot=slot: e.activation(
                    out=PT[slot][:, 0:pw], in_=ps[b][:, 0:pw], func=AF.Exp, scale=ATTN_SCALE),
                    r=[("ps", b)], w=[("PT", slot)])
                return (c, i, tt, slot, p0, pw)

            def emit_pv(info):
                c, i, tt, slot, p0, pw = info
                for jj in range(pw // 128):
                    j = i + p0 // 128 + jj
                    ob = O_BANK[j // 3]
                    oc = (j % 3) * 130
                    sc.op("pe", lambda e, ob=ob, oc=oc, j=j, jj=jj, tt=tt, slot=slot, c=c, h=h: e.matmul(
                        ps[ob][:, oc:oc + 129], PT[slot][:, jj * 128:(jj + 1) * 128], Vaug[:, tt, h, 0:129],
                        start=(c == 0 and j % 3 == 0), stop=(c == 2 * j + 1), skip_group_check=True),
                        r=[("PT", slot), ("V",), ("Vones",)], w=[("ps", ob)])

            LOOK = 2
            infos = []
            for u, uinfo in enumerate(units):
                infos.append(emit_st(u, uinfo))
                if u == 2 and deferred:
                    deferred.pop()()
                if u >= LOOK:
                    emit_pv(infos[u - LOOK])
            for u in range(len(units) - LOOK, len(units)):
                emit_pv(infos[u])
            for ob in range(3):
                nj = 3 if ob < 2 else 2
                src = ps[O_BANK[ob]][:, 0:nj * 130].rearrange("p (j d) -> p j d", j=nj, d=130)
                sc.op("dve", lambda e, ob=ob, nj=nj, src=src: e.reciprocal(
                    out=rcp[:, ob * 3:ob * 3 + nj].rearrange("p (j o) -> p j o", o=1), in_=src[:, :, 128:129]),
                    r=[("ps", O_BANK[ob])], pw=[("rcp",)])
            for j in range(8):
                ob = O_BANK[j // 3]
                oc = (j % 3) * 130
                if (j // 3) % 2 == 0:
                    sc.op("act", lambda e, ob=ob, oc=oc, j=j: e.activation(
                        out=Osb[:, j, :], in_=ps[ob][:, oc:oc + 128], func=AF.Identity, scale=rcp[:, j:j + 1]),
                        r=[("ps", ob), ("rcp",)], pw=[("Osb",)])
                else:
                    sc.op("dve", lambda e, ob=ob, oc=oc, j=j: e.tensor_scalar(
                        out=Osb[:, j, :], in0=ps[ob][:, oc:oc + 128], scalar1=rcp[:, j:j + 1], scalar2=None, op0=ALU.mult),
                        r=[("ps", ob), ("rcp",)], pw=[("Osb",)])
            def fin_tr(hg=hg):
                trv = ps[TR_BANK][:, :].bitcast(BF16)
                for j in range(8):
                    sc.op("pe", lambda e, j=j, trv=trv: e.transpose(trv[:, j * 128:(j + 1) * 128], Osb[:, j, :], ident_bf),
                          r=[("Osb",), ("identbf",)], w=[("ps", TR_BANK)])
                sc.op("dve", lambda e, hg=hg, trv=trv: e.tensor_copy(out=attnT[:, hg, :], in_=trv),
                      r=[("ps", TR_BANK)], w=[("attnT", hg)])
            deferred.append(fin_tr)
        adaln(ada_rest[hp * 2:hp * 2 + 2], 7)
    while deferred:
        deferred.pop()()
    sc.op("dve", lambda e: e.tensor_scalar(out=sc2p, in0=modT[:, 64:80], scalar1=1.0, scalar2=None, op0=ALU.add),
          r=[("modT", 64), ("modT", 72)], w=[("mod2",)])

    def conv_parts(m):
        def part(off, src):
            def dst(sl):
                return sl[:, off:off + KC * 128].rearrange("p (k n) -> p k n", k=KC, n=128)
            return (dst, src.rearrange("(k p) n -> p k n", p=128))
        return [part(0, w_in[:, 3136 + m * 128:3136 + (m + 1) * 128]),
                part(2048, w_in[:, 5184 + m * 128:5184 + (m + 1) * 128]),
                part(4096, w_in[:, 1088 + m * 128:1088 + (m + 1) * 128])]
    pre_conv = [wload(conv_parts(0)), wload(conv_parts(1))] if stop is None or stop not in ("p3",) else []

    if stop_here("p3"):
        add_dump("attnT", attnT, [128, NH, 1024], BF16)
        add_dump("modT", modT, [128, 96], F32)
        add_dump("kvnT", kvnT, [128, 4, 2048], BF16)
        add_dump("qnT", qnT, [128, 4, 1024], BF16)
        add_dump("krT", krT, [128, 2048], BF16)
        add_dump("cosT", cosT, [128, 2048], F32)
        add_dump("sinT", sinT, [128, 2048], F32)
        return finish(nc, sc, es, dump_out)

    sc.barrier()
    MODALL = [("modT", c_) for c_ in range(32, 96, 8)]
    O_C = O_XS
    mbT = view(O_C, [128, KC, 1024], BF16)
    ybT = view(O_C + 32768, [128, KC, 1024], BF16)
    O_CT = O_C + 65536
    cbuf = view(O_CT, [128, 1040])
    zbuf = view(O_CT + 4160, [128, 8, 130])
    cacc = view(O_CT + 8320, [128, 8, 128])
    sgt = view(O_CT + 12416, [128, 1024])
    assert O_CT + 16512 <= ARENA
    uO = ("uT", "own", 0), ("uT", "own", 1), ("uT", "halo")
    for m in range(KC):
        s_ = pre_conv[m] if m < len(pre_conv) else wload(conv_parts(m))
        wc_, wx_, wb_ = wview(s_, 128, KC, 0), wview(s_, 128, KC, 2048), wview(s_, 128, KC, 4096)
        WK = ("w", s_)
        for (wv2, b0, hc) in ((wc_, 0, 6), (wx_, 2, 7), (wb_, 4, None)):
            for k in range(KC):
                for n in range(2):
                    sc.op("pe", lambda e, wv2=wv2, b0=b0, n=n, k=k: e.matmul(
                        ps[b0 + n][:, :], wv2[:, k, :], uT_own[:, k, n * 512:(n + 1) * 512],
                        start=(k == 0), stop=(k == KC - 1)), r=[WK, ("uT", "own", n)], w=[("ps", b0 + n)])
                if hc is not None:
                    sc.op("pe", lambda e, wv2=wv2, hc=hc, k=k: e.matmul(
                        ps[hc][:, 0:16], wv2[:, k, :], uT_own[:, k, 1024:1040],
                        start=(k == 0), stop=(k == KC - 1)), r=[WK, ("uT", "halo")], w=[("ps", hc)])
            if b0 == 0:
                for n in range(2):
                    sc.op("act", lambda e, n=n: e.activation(out=cbuf[:, n * 512:(n + 1) * 512], in_=ps[n][:, :], func=AF.Copy),
                          r=[("ps", n)], w=[("cbuf",)])
                sc.op("act", lambda e: e.activation(out=cbuf[:, 1024:1040], in_=ps[6][:, 0:16], func=AF.Copy),
                      r=[("ps", 6)], w=[("cbuf",)])
            elif b0 == 2:
                for n in range(2):
                    sc.op("dve", lambda e, n=n: e.tensor_tensor(
                        out=zbuf[:, n * 4:(n + 1) * 4, 2:130],
                        in0=ps[2 + n][:, :].rearrange("p (j t) -> p j t", j=4, t=128),
                        in1=cbuf[:, n * 512:(n + 1) * 512].rearrange("p (j t) -> p j t", j=4, t=128), op=ALU.mult),
                        r=[("ps", 2 + n), ("cbuf",)], w=[("zbuf",)])
                sc.op("dve", lambda e: e.tensor_tensor(
                    out=zbuf[:, :, 0:2], in0=ps[7][:, 0:16].rearrange("p (j t) -> p j t", j=8, t=2),
                    in1=cbuf[:, 1024:1040].rearrange("p (j t) -> p j t", j=8, t=2), op=ALU.mult),
                    r=[("ps", 7), ("cbuf",)], w=[("zbuf",)])
                sc.op("dve", lambda e: e.tensor_tensor(
                    out=zbuf[:, :, 0:2], in0=zbuf[:, :, 0:2], in1=hvalid.rearrange("p (j t) -> p j t", j=8, t=2), op=ALU.mult),
                    r=[("zbuf",), CONST], w=[("zbuf",)])
                sc.op("dve", lambda e, m=m: e.tensor_scalar(
                    out=cacc, in0=zbuf[:, :, 0:128], scalar1=wconv[:, m * 3:m * 3 + 1], scalar2=None, op0=ALU.mult),
                    r=[("zbuf",), CONST], w=[("cacc",)])
                for kk in (1, 2):
                    sc.op("dve", lambda e, m=m, kk=kk: e.scalar_tensor_tensor(
                        out=cacc, in0=zbuf[:, :, kk:kk + 128], scalar=wconv[:, m * 3 + kk:m * 3 + kk + 1], in1=cacc,
                        op0=ALU.mult, op1=ALU.add), r=[("zbuf",), ("cacc",), CONST], w=[("cacc",)])
            else:
                for n in range(2):
                    sc.op("dve", lambda e, n=n, m=m: e.tensor_tensor(
                        out=ybT[:, m, n * 512:(n + 1) * 512], in0=ps[4 + n][:, :],
                        in1=cacc[:, n * 4:(n + 1) * 4, :].rearrange("p j t -> p (j t)"), op=ALU.mult),
                        r=[("ps", 4 + n), ("cacc",)], w=[("ybT", m)])

    def gated_proj(w_main, src_act, src_keys, gate_col0, out_fn, tagk):
        for tp in range(8):
            s_w = wload(wtile_pair(w_in[:, gate_col0 + tp * 256:gate_col0 + (tp + 1) * 256],
                                   w_main[:, tp * 256:(tp + 1) * 256]))
            wv_ = wview(s_w, 512)
            for mm in range(2):
                m = tp * 2 + mm
                par = (m % 2) * 4
                for k in range(KC):
                    for n in range(2):
                        sc.op("pe", lambda e, par=par, n=n, k=k, mm=mm, wv_=wv_: e.matmul(
                            ps[par + n][:, :], wv_[:, k, mm * 128:(mm + 1) * 128], uT_own[:, k, n * 512:(n + 1) * 512],
                            start=(k == 0), stop=(k == KC - 1)), r=[("w", s_w), ("uT", "own", n)], w=[("ps", par + n)])
                for n in range(2):
                    sc.op("act", lambda e, par=par, n=n: e.activation(
                        out=sgt[:, n * 512:(n + 1) * 512], in_=ps[par + n][:, :], func=AF.Sigmoid),
                        r=[("ps", par + n)], w=[("sgt", n)])
                for k in range(KC):
                    for n in range(2):
                        sc.op("pe", lambda e, par=par, n=n, k=k, mm=mm, wv_=wv_: e.matmul(
                            ps[par + 2 + n][:, :], wv_[:, k, 256 + mm * 128:256 + (mm + 1) * 128], src_act[:, k, n * 512:(n + 1) * 512],
                            start=(k == 0), stop=(k == KC - 1)), r=[("w", s_w)] + src_keys, w=[("ps", par + 2 + n)])
                for n in range(2):
                    out_fn(m, n, ps[par + 2 + n][:, :], ("ps", par + 2 + n))

    def out_b(m, n, psrc, pkey):
        sc.op("dve", lambda e: e.tensor_tensor(out=mbT[:, m, n * 512:(n + 1) * 512], in0=psrc,
                                               in1=sgt[:, n * 512:(n + 1) * 512], op=ALU.mult),
              r=[pkey, ("sgt", n)], w=[("mbT", m)])

    gated_proj(w_o_b, ybT, [("ybT", m_) for m_ in range(KC)], 9280, out_b, "b")

    def out_a(m, n, psrc, pkey):
        sc.op("dve", lambda e: e.tensor_tensor(out=sgt[:, n * 512:(n + 1) * 512], in0=psrc,
                                               in1=sgt[:, n * 512:(n + 1) * 512], op=ALU.mult),
              r=[pkey, ("sgt", n)], w=[("sgt", n)])
        sc.op("dve", lambda e: e.tensor_tensor(out=mbT[:, m, n * 512:(n + 1) * 512], in0=sgt[:, n * 512:(n + 1) * 512],
                                               in1=mbT[:, m, n * 512:(n + 1) * 512], op=ALU.add),
              r=[("sgt", n), ("mbT", m)], w=[("mbT", m)])

    gated_proj(w_o_a, attnT, [("attnT", h_) for h_ in range(NH)], 7232, out_a, "a")
    mergedT = mbT

    pre_wo = [wload([wtile_std(w_o[:, tg * 512:(tg + 1) * 512], 512)]) for tg in range(2)]

    if stop_here("p5"):
        add_dump("mergedT", mergedT, [128, KC, 1024], BF16)
        add_dump("ybT", ybT, [128, KC, 1024], BF16)
        return finish(nc, sc, es, dump_out)

    sc.barrier()
    r1buf = view(B0, [128, 8, 2048])
    assert B0 + 65536 <= O_C
    O_M = O_C + 32768
    O_U2 = ARENA - 32768
    gbc = view(O_M, [128, 2048])
    bbc = view(O_M + 8192, [128, 2048])
    mixbuf = view(O_M + 16384, [128, 2, 1024], BF16)
    stats1 = view(O_M + 20480, [128, 8, 8, 6])
    mv1 = view(O_M + 22016, [128, 8, 2])
    sd1 = view(O_M + 22080, [128, 8])
    rs1 = view(O_M + 22112, [128, 8])
    nm1 = view(O_M + 22144, [128, 8])
    assert O_M + 22176 <= O_U2
    u2T = view(O_U2, [128, KC, 1024], BF16)
    sc.dma("sp", gbc, ln1_g.to_broadcast([128, D]), w=[("gbc",)], sem=("gbc",))
    sc.dma("sp", bbc, ln1_b.to_broadcast([128, D]), w=[("bbc",)], sem=("bbc",))
    for tt in range(8):
        sc.dma("sp", r1buf[:, tt, :], x_all[tt * 128:(tt + 1) * 128, :], w=[("r1ld",)], sem=("r1ld",), nodep=(tt > 0))
    R1LD = ("r1ld",)
    sc.op("dve", lambda e: e.tensor_tensor(out=G2c, in0=g1c, in1=sc2p, op=ALU.mult), r=[CONST, ("mod2",)], w=[("G2c",)])
    sc.op("dve", lambda e: e.tensor_tensor(out=B2c, in0=b1c, in1=sc2p, op=ALU.mult), r=[CONST, ("mod2",)], w=[("B2c",)])
    sc.op("dve", lambda e: e.tensor_tensor(out=B2c, in0=B2c, in1=modT[:, 48:64], op=ALU.add), r=[("B2c",)] + MODALL, w=[("B2c",)])

    def mix_transposes(mp):
        for tt in range(8):
            bk = 6 + tt // 4
            trv = ps[bk][:, :].bitcast(BF16)
            for mo in range(2):
                c0 = (tt % 4) * 256 + mo * 128
                sc.op("pe", lambda e, trv=trv, c0=c0, mo=mo, tt=tt: e.transpose(
                    trv[:, c0:c0 + 128], mixbuf[:, mo, tt * 128:(tt + 1) * 128], ident_bf),
                    r=[("mixbuf", mo), ("identbf",)], w=[("ps", bk)])
        for tt in range(8):
            bk = 6 + tt // 4
            trv = ps[bk][:, :].bitcast(BF16)
            c0 = (tt % 4) * 256
            dstv = r1buf[:, tt, mp * 256:(mp + 1) * 256]
            sc.op("dve", lambda e, trv=trv, c0=c0, dstv=dstv: e.scalar_tensor_tensor(
                out=dstv, in0=dstv, scalar=ALPHA, in1=trv[:, c0:c0 + 256], op0=ALU.mult, op1=ALU.add),
                r=[("ps", bk), R1LD], pw=[("r1", tt)])
            sc.op("dve", lambda e, dstv=dstv, tt=tt, mp=mp: e.bn_stats(out=stats1[:, tt, mp, :], in_=dstv),
                  r=[("r1", tt)], pw=[("st1", tt)])

    pending_tr = None
    for tg in range(4):
        s_m = pre_wo[tg] if tg < len(pre_wo) else wload([wtile_std(w_o[:, tg * 512:(tg + 1) * 512], 512)])
        if tg == 3:
            pre_ffn = [wload(wtile_pair(w_ffn_in[:, tp_ * 256:(tp_ + 1) * 256], w_ffn_in[:, DFF + tp_ * 256:DFF + (tp_ + 1) * 256]))
                       for tp_ in range(2)]
        wm_ = wview(s_m, 512)
        for pr in range(2):
            mp = tg * 2 + pr

            def bank(mo, n, mp=mp):
                return (4 * mp + 2 * mo + n) % 6
            for mo in range(2):
                mm = pr * 2 + mo
                for k in range(KC):
                    for n in range(2):
                        b_ = bank(mo, n)
                        sc.op("pe", lambda e, b_=b_, n=n, k=k, mm=mm, wm_=wm_: e.matmul(
                            ps[b_][:, :], wm_[:, k, mm * 128:(mm + 1) * 128], mergedT[:, k, n * 512:(n + 1) * 512],
                            start=(k == 0), stop=(k == KC - 1)), r=[("w", s_m), ("mbT", k)], w=[("ps", b_)])
                if mo == 0 and pending_tr is not None:
                    mix_transposes(pending_tr)
                    pending_tr = None
            for mo in range(2):
                m = mp * 2 + mo
                for n in range(2):
                    b_ = bank(mo, n)
                    if n == 0:
                        sc.op("act", lambda e, b_=b_, n=n, m=m, mo=mo: e.activation(
                            out=mixbuf[:, mo, n * 512:(n + 1) * 512], in_=ps[b_][:, :], func=AF.Identity, scale=modT[:, 32 + m:33 + m]),
                            r=[("ps", b_)] + MODALL, pw=[("mixbuf", mo)])
                    else:
                        sc.op("dve", lambda e, b_=b_, n=n, m=m, mo=mo: e.tensor_scalar(
                            out=mixbuf[:, mo, n * 512:(n + 1) * 512], in0=ps[b_][:, :], scalar1=modT[:, 32 + m:33 + m],
                            scalar2=None, op0=ALU.mult), r=[("ps", b_)] + MODALL, pw=[("mixbuf", mo)])
            pending_tr = mp
    mix_transposes(pending_tr)
    for tt in range(8):
        sc.op("dve", lambda e, tt=tt: e.bn_aggr(out=mv1[:, tt, :], in_=stats1[:, tt, :, :].rearrange("p a b -> p (a b)")),
              r=[("st1", tt)], w=[("mv1",)])
    sc.op("dve", lambda e: e.tensor_scalar(out=sd1, in0=mv1[:, :, 1], scalar1=LN_EPS, scalar2=None, op0=ALU.add),
          r=[("mv1",)], w=[("sd1",)])
    sc.op("act", lambda e: e.activation(out=sd1, in_=sd1, func=AF.Sqrt), r=[("sd1",)], w=[("sd1",)])
    sc.op("dve", lambda e: e.reciprocal(out=rs1, in_=sd1), r=[("sd1",)], w=[("rs1",)])
    sc.op("dve", lambda e: e.scalar_tensor_tensor(out=nm1, in0=mv1[:, :, 0], scalar=-1.0, in1=rs1, op0=ALU.mult, op1=ALU.mult),
          r=[("mv1",), ("rs1",)], w=[("nm1",)])
    NF = DFF // 128
    actT = view(B0, [128, NF, 1024], BF16)
    O_G = B0 + NF * 2048
    sgt2 = view(O_G, [128, 1024])
    assert O_G + 4096 <= O_U2
    ALIAS_MB = [("mbT", k_) for k_ in range(KC)]

    def actT_alias(m):
        if m < 32:
            return [("r1", m // 4), ("r1p", m // 4)]
        return ALIAS_MB

    def ffn_evac(m, n, b_gate_or_up, is_gate):
        if is_gate:
            sc.op("act", lambda e: e.activation(out=sgt2[:, n * 512:(n + 1) * 512], in_=ps[b_gate_or_up][:, :], func=AF.Silu),
                  r=[("ps", b_gate_or_up)], w=[("sgt2", n)] + ALIAS_MB)
        else:
            sc.op("dve", lambda e: e.tensor_tensor(out=actT[:, m, n * 512:(n + 1) * 512], in0=ps[b_gate_or_up][:, :],
                                                   in1=sgt2[:, n * 512:(n + 1) * 512], op=ALU.mult),
                  r=[("ps", b_gate_or_up), ("sgt2", n)], w=[("actT", m)] + actT_alias(m))

    def ffn_half(s_w, m, mm, n, par):
        wv_ = wview(s_w, 512)
        for (c0, boff) in ((0, 0), (256, 2)):
            b_ = par + boff + n
            for k in range(KC):
                sc.op("pe", lambda e, b_=b_, k=k, c0=c0, wv_=wv_: e.matmul(
                    ps[b_][:, :], wv_[:, k, c0 + mm * 128:c0 + (mm + 1) * 128], u2T[:, k, n * 512:(n + 1) * 512],
                    start=(k == 0), stop=(k == KC - 1)), r=[("w", s_w), ("u2T", n)], w=[("ps", b_)])
            ffn_evac(m, n, b_, boff == 0)

    def ffn_chunk(s_w, m, mm, par):
        wv_ = wview(s_w, 512)
        for (c0, boff) in ((0, 0), (256, 2)):
            for k in range(KC):
                for n in range(2):
                    sc.op("pe", lambda e, n=n, k=k, c0=c0, boff=boff, wv_=wv_: e.matmul(
                        ps[par + boff + n][:, :], wv_[:, k, c0 + mm * 128:c0 + (mm + 1) * 128], u2T[:, k, n * 512:(n + 1) * 512],
                        start=(k == 0), stop=(k == KC - 1)), r=[("w", s_w), ("u2T", n)], w=[("ps", par + boff + n)])
            for n in range(2):
                ffn_evac(m, n, par + boff + n, boff == 0)

    for tt in range(8):
        rb = r1buf[:, tt, :]
        sc.op("act", lambda e, rb=rb, tt=tt: e.activation(out=rb, in_=rb, func=AF.Identity, bias=nm1[:, tt:tt + 1], scale=rs1[:, tt:tt + 1]),
              r=[("r1", tt), ("rs1",), ("nm1",)], w=[("r1", tt)])
        for q in range(4):
            bk = (tt % 2) * 4 + q
            for j in range(4):
                m = q * 4 + j
                sc.op("pe", lambda e, bk=bk, j=j, m=m, rb=rb: e.transpose(
                    ps[bk][:, j * 128:(j + 1) * 128], rb[:, m * 128:(m + 1) * 128], ident),
                    r=[("r1", tt), ("r1p", tt), CONST], w=[("ps", bk)])
            for j in range(4):
                m = q * 4 + j
                dst = u2T[:, m, tt * 128:(tt + 1) * 128]
                src = ps[bk][:, j * 128:(j + 1) * 128]
                if q % 2 == 0:
                    sc.op("act", lambda e, dst=dst, src=src, m=m: e.activation(
                        out=dst, in_=src, func=AF.Identity, bias=B2c[:, m:m + 1], scale=G2c[:, m:m + 1]),
                        r=[("ps", bk), ("G2c",), ("B2c",)], pw=[("u2T", tt // 4)])
                else:
                    sc.op("dve", lambda e, dst=dst, src=src, m=m: e.tensor_scalar(
                        out=dst, in0=src, scalar1=G2c[:, m:m + 1], scalar2=B2c[:, m:m + 1],
                        op0=ALU.mult, op1=ALU.add), r=[("ps", bk), ("G2c",), ("B2c",)], pw=[("u2T", tt // 4)])
        sc.op("pool", lambda e, rb=rb: e.tensor_tensor(out=rb, in0=rb, in1=gbc, op=ALU.mult),
              r=[("r1", tt), ("gbc",)], w=[("r1p", tt)])
        sc.op("pool", lambda e, rb=rb: e.tensor_tensor(out=rb, in0=rb, in1=bbc, op=ALU.add),
              r=[("r1p", tt), ("bbc",)], w=[("r1p", tt)])
        sc.dma("pool", x1d[tt * 128:(tt + 1) * 128, :], rb, r=[("r1", tt), ("r1p", tt)], w=[("x1d", tt)], sem=("x1st", tt))
        if tt >= 4:
            idx = tt - 4
            ffn_half(pre_ffn[idx // 2], idx, idx % 2, 0, ((tt + 1) % 2) * 4)

    if stop_here("p6"):
        add_dump("u2T", u2T, [128, KC, 1024], BF16)
        return finish(nc, sc, es, dump_out)

    for idx in range(4):
        ffn_half(pre_ffn[idx // 2], idx, idx % 2, 1, (idx % 2) * 4)
    for tp in range(2, NF // 2):
        s_w = wload(wtile_pair(w_ffn_in[:, tp * 256:(tp + 1) * 256], w_ffn_in[:, DFF + tp * 256:DFF + (tp + 1) * 256]))
        for mm in range(2):
            m = tp * 2 + mm
            ffn_chunk(s_w, m, mm, (m % 2) * 4)

    HK = NF // 2

    def fo_parts(mp, k0):
        def dst(sl):
            return sl[:, 0:HK * 256].rearrange("p (k n) -> p k n", k=HK, n=256)
        cols = w_ffn_out[:, mp * 256:(mp + 1) * 256]
        return [(dst, cols[k0 * 128:(k0 + HK) * 128, :].rearrange("(k p) n -> p k n", p=128))]
    n0 = wstate["n"]
    slotT = (n0 + 15) % NSLOT
    slotZ = (n0 + 16) % NSLOT
    pre_fo = (wload(fo_parts(0, 0)), wload(fo_parts(0, HK)))
    sc.barrier()
    O_R2 = ARENA - 65536
    assert O_G + 1536 + 64 + 96 <= O_R2
    r2buf = view(O_R2, [128, 8, 2048])
    stats2 = view(O_G, [128, 8, 8, 6])
    mv2 = view(O_G + 1536, [128, 8, 2])
    sd2 = view(O_G + 1600, [128, 8])
    rs2 = view(O_G + 1632, [128, 8])
    nm2 = view(O_G + 1664, [128, 8])
    fobuf = wslot[slotT][:, HK * 256:HK * 256 + 2048].rearrange("p (a b) -> p a b", a=2, b=1024)
    gb2 = view(4 * KB + slotZ * 16 * KB, [128, 4096])
    gbc2, bbc2 = gb2[:, 0:2048], gb2[:, 2048:4096]
    for tt in range(8):
        sc.dma("sp", r2buf[:, tt, :], x1d[tt * 128:(tt + 1) * 128, :], w=[("r2ld",)], sem=("r2ld",), nodep=(tt > 0))
    R2LD = ("r2ld",)

    def fo_transposes(mp):
        for tt in range(8):
            bk = 6 + tt // 4
            trv = ps[bk][:, :].bitcast(BF16)
            for mo in range(2):
                c0 = (tt % 4) * 256 + mo * 128
                sc.op("pe", lambda e, trv=trv, c0=c0, mo=mo, tt=tt: e.transpose(
                    trv[:, c0:c0 + 128], fobuf[:, mo, tt * 128:(tt + 1) * 128], ident_bf),
                    r=[("fobuf", mo), ("identbf",)], w=[("ps", bk)])
        for tt in range(8):
            bk = 6 + tt // 4
            trv = ps[bk][:, :].bitcast(BF16)
            c0 = (tt % 4) * 256
            dstv = r2buf[:, tt, mp * 256:(mp + 1) * 256]
            sc.op("dve", lambda e, trv=trv, c0=c0, dstv=dstv: e.scalar_tensor_tensor(
                out=dstv, in0=dstv, scalar=ALPHA, in1=trv[:, c0:c0 + 256], op0=ALU.mult, op1=ALU.add),
                r=[("ps", bk), R2LD], pw=[("r2", tt)])
            sc.op("dve", lambda e, dstv=dstv, tt=tt, mp=mp: e.bn_stats(out=stats2[:, tt, mp, :], in_=dstv),
                  r=[("r2", tt)], pw=[("st2", tt)])

    pending_tr = None
    for mp in range(KC // 2):
        if mp == 0:
            s_a, s_b = pre_fo
        else:
            s_a = wload(fo_parts(mp, 0))
            s_b = wload(fo_parts(mp, HK))
        wa_ = wview(s_a, 256, HK)
        wb_ = wview(s_b, 256, HK)

        def bank(mo, n, mp=mp):
            return (4 * mp + 2 * mo + n) % 6
        for ti, (wv2, skey, k0) in enumerate(((wa_, ("w", s_a), 0), (wb_, ("w", s_b), HK))):
            for mo in range(2):
                for kk in range(HK):
                    k = k0 + kk
                    for n in range(2):
                        b_ = bank(mo, n)
                        sc.op("pe", lambda e, b_=b_, mo=mo, n=n, k=k, kk=kk, wv2=wv2: e.matmul(
                            ps[b_][:, :], wv2[:, kk, mo * 128:(mo + 1) * 128], actT[:, k, n * 512:(n + 1) * 512],
                            start=(k == 0), stop=(k == NF - 1)), r=[skey, ("actT", k)], w=[("ps", b_)])
            if ti == 0 and pending_tr is not None:
                fo_transposes(pending_tr)
                pending_tr = None
        for mo in range(2):
            m = mp * 2 + mo
            for n in range(2):
                b_ = bank(mo, n)
                if n == 0:
                    sc.op("act", lambda e, b_=b_, n=n, m=m, mo=mo: e.activation(
                        out=fobuf[:, mo, n * 512:(n + 1) * 512], in_=ps[b_][:, :], func=AF.Identity, scale=modT[:, 80 + m:81 + m]),
                        r=[("ps", b_)] + MODALL, pw=[("fobuf", mo)])
                else:
                    sc.op("dve", lambda e, b_=b_, n=n, m=m, mo=mo: e.tensor_scalar(
                        out=fobuf[:, mo, n * 512:(n + 1) * 512], in0=ps[b_][:, :], scalar1=modT[:, 80 + m:81 + m],
                        scalar2=None, op0=ALU.mult), r=[("ps", b_)] + MODALL, pw=[("fobuf", mo)])
        pending_tr = mp
    sc.dma("sp", gbc2, ln2_g.to_broadcast([128, D]), w=[("w", slotZ)], sem=("w", slotZ))
    sc.dma("sp", bbc2, ln2_b.to_broadcast([128, D]), w=[("w", slotZ)], sem=("w", slotZ), nodep=True)
    fo_transposes(pending_tr)
    for tt in range(8):
        sc.op("dve", lambda e, tt=tt: e.bn_aggr(out=mv2[:, tt, :], in_=stats2[:, tt, :, :].rearrange("p a b -> p (a b)")),
              r=[("st2", tt)], w=[("mv2",)])
    sc.op("dve", lambda e: e.tensor_scalar(out=sd2, in0=mv2[:, :, 1], scalar1=LN_EPS, scalar2=None, op0=ALU.add),
          r=[("mv2",)], w=[("sd2",)])
    sc.op("act", lambda e: e.activation(out=sd2, in_=sd2, func=AF.Sqrt), r=[("sd2",)], w=[("sd2",)])
    sc.op("dve", lambda e: e.reciprocal(out=rs2, in_=sd2), r=[("sd2",)], w=[("rs2",)])
    sc.op("dve", lambda e: e.scalar_tensor_tensor(out=nm2, in0=mv2[:, :, 0], scalar=-1.0, in1=rs2, op0=ALU.mult, op1=ALU.mult),
          r=[("mv2",), ("rs2",)], w=[("nm2",)])
    for q in range(1, 4):
        sc.op("act", lambda e, q=q: e.activation(out=ps[q][:, :], in_=gbc2[:, q * 512:(q + 1) * 512], func=AF.Copy),
              r=[("w", slotZ)], w=[("ps", q)])
        sc.op("act", lambda e, q=q: e.activation(out=ps[4 + q][:, :], in_=bbc2[:, q * 512:(q + 1) * 512], func=AF.Copy),
              r=[("w", slotZ)], w=[("ps", 4 + q)])
    PC = 512
    for tt in range(8):
        rb = r2buf[:, tt, :]
        sc.op("act", lambda e, rb=rb, tt=tt: e.activation(out=rb, in_=rb, func=AF.Identity, bias=nm2[:, tt:tt + 1], scale=rs2[:, tt:tt + 1]),
              r=[("r2", tt), ("rs2",), ("nm2",)], w=[("r2", tt)])
        sc.op("pool", lambda e, rb=rb: e.tensor_tensor(out=rb[:, 0:PC], in0=rb[:, 0:PC], in1=gbc2[:, 0:PC], op=ALU.mult),
              r=[("r2", tt), ("w", slotZ)], w=[("r2p", tt)])
        sc.op("pool", lambda e, rb=rb: e.tensor_tensor(out=rb[:, 0:PC], in0=rb[:, 0:PC], in1=bbc2[:, 0:PC], op=ALU.add),
              r=[("r2p", tt), ("w", slotZ)], w=[("r2p", tt)])
        for q in range(1, 4):
            cs = slice(q * 512, (q + 1) * 512)
            sc.op("dve", lambda e, rb=rb, cs=cs, q=q: e.tensor_tensor(out=rb[:, cs], in0=rb[:, cs], in1=ps[q][:, :], op=ALU.mult),
                  r=[("r2", tt), ("ps", q)], w=[("r2d", tt, q)])
            sc.op("dve", lambda e, rb=rb, cs=cs, q=q: e.tensor_tensor(out=rb[:, cs], in0=rb[:, cs], in1=ps[4 + q][:, :], op=ALU.add),
                  r=[("r2d", tt, q), ("ps", 4 + q)], w=[("r2d", tt, q)])
        sc.dma("pool", y[tt * 128:(tt + 1) * 128, :], rb,
               r=[("r2", tt), ("r2p", tt)] + [("r2d", tt, q) for q in range(1, 4)], w=[("y", tt)], sem=("yst", tt % 3))
    return finish(nc, sc, es, dump_out)


def finish(nc, sc, es, dump_out):
    sc.barrier()
    sc.emit(nc, es)
    es.close()
    return nc, dump_out


def _own_blocks(parity):
    return [2 * j + parity for j in range(8)]


def prep_inputs(inp):
    f = np.float32
    x = np.asarray(inp["x"], f)
    c = np.asarray(inp["c"], f)
    pos = np.asarray(inp["positions"], np.int32)
    w_in = np.ascontiguousarray(np.asarray(inp["w_in"], f)[0])
    w_q_b = np.asarray(inp["w_q_b"], f)[0]
    w_kv_b = np.asarray(inp["w_kv_b"], f)[0]
    rope = w_in[:, 1024:1088]
    swap = np.concatenate([rope[:, 32:64], rope[:, 0:32]], axis=1)
    w_rope = np.ascontiguousarray(np.concatenate([rope, rope, swap, swap], axis=1))
    q3 = w_q_b.reshape(512, NH, 192)
    q_nope = q3[:, :, 0:128].reshape(512, NH * 128)
    q_rope = q3[:, :, 128:192]
    q_swap = np.concatenate([q_rope[:, :, 32:64], q_rope[:, :, 0:32]], axis=2)
    w_qb = np.ascontiguousarray(np.concatenate([q_nope, q_rope.reshape(512, NH * 64), q_swap.reshape(512, NH * 64)], axis=1))
    kv3 = w_kv_b.reshape(512, NH, 256)
    w_kvb = np.ascontiguousarray(np.concatenate([kv3[:, :, 0:128].reshape(512, NH * 128),
                                                 kv3[:, :, 128:256].reshape(512, NH * 128)], axis=1))

    def col(v, n):
        return np.ascontiguousarray(np.asarray(v, f).reshape(n, 128).T)

    shared = {
        "w_ada": np.ascontiguousarray(np.asarray(inp["w_ada"], f)[0]),
        "b_ada_col": col(inp["b_ada"][0], 96),
        "w_in": w_in,
        "w_rope": w_rope,
        "w_qb": w_qb,
        "w_kvb": w_kvb,
        "w_o_a": np.ascontiguousarray(np.asarray(inp["w_o_a"], f)[0]),
        "w_o_b": np.ascontiguousarray(np.asarray(inp["w_o_b"], f)[0]),
        "w_o": np.ascontiguousarray(np.asarray(inp["w_o"], f)[0]),
        "w_ffn_in": np.ascontiguousarray(np.asarray(inp["w_ffn_in"], f)[0]),
        "w_ffn_out": np.ascontiguousarray(np.asarray(inp["w_ffn_out"], f)[0]),
        "gq_col": col(inp["g_q_a"][0], 4),
        "gkv_col": col(inp["g_kv_a"][0], 4),
        "wconv_col": np.ascontiguousarray(np.asarray(inp["w_conv"], f)[0].reshape(3, KC, 128).transpose(2, 1, 0).reshape(128, KC * 3)),
        "ln1g_col": col(inp["ln1_g"][0], KC),
        "ln1b_col": col(inp["ln1_b"][0], KC),
        "ln1_g": np.asarray(inp["ln1_g"], f).reshape(1, D),
        "ln1_b": np.asarray(inp["ln1_b"], f).reshape(1, D),
        "ln2_g": np.asarray(inp["ln2_g"], f).reshape(1, D),
        "ln2_b": np.asarray(inp["ln2_b"], f).reshape(1, D),
        "ident": np.eye(128, dtype=f),
    }
    kk = np.arange(128)[:, None] // 64
    qq = np.arange(128)[None, :] // 64
    shared["mask_own"] = np.where(kk <= qq, 0.0, NEG).astype(f)
    p = np.arange(128)
    inv_freq = (1.0 / (np.float32(10000.0) ** (np.arange(0, 64, 2, dtype=np.float32) / np.float32(64)))).astype(f)
    shared["invf_col"] = inv_freq[p % 32].reshape(128, 1).astype(f)
    shared["sgn_col"] = np.where((p % 64) < 32, -1.0, 1.0).reshape(128, 1).astype(f)

    in_maps = []
    for core in range(8):
        b, par = core // 2, core % 2
        own = _own_blocks(par)
        oth = _own_blocks(1 - par)
        order = np.concatenate([np.arange(g * 128, (g + 1) * 128) for g in own + oth])
        xa = np.ascontiguousarray(x[b][order])
        halo = np.zeros((16, D), f)
        hv = np.zeros((1, 16), f)
        for j, g in enumerate(own):
            if g > 0:
                halo[2 * j:2 * j + 2] = x[b][g * 128 - 2:g * 128]
                hv[0, 2 * j:2 * j + 2] = 1.0
        m = dict(shared)
        m["x_all"] = xa
        m["x_halo"] = halo
        m["x_allT"] = np.ascontiguousarray(np.concatenate([xa[:1024], halo, xa[1024:]], axis=0).T)
        m["halo_valid"] = hv
        m["pos"] = np.ascontiguousarray(pos[b][order].reshape(1, S))
        m["c_col"] = col(c[b], KC)
        m["mask_oth"] = np.full((128, 128), 0.0 if par == 1 else NEG, f)
        in_maps.append(m)
    return in_maps


_CACHE = {}


def kernel(**inputs):
    in_maps = prep_inputs(inputs)
    if "nc" not in _CACHE:
        _CACHE["nc"] = build_program(STOP, DUMPS)
    nc, dump_out = _CACHE["nc"]
    res = run_bass_kernel_spmd(nc, in_maps[:NCORES], core_ids=list(range(NCORES)))
    if STOP is not None:
        return res
    out = np.zeros((4, S, D), np.float32)
    for core in range(8):
        b, par = core // 2, core % 2
        yc = np.asarray(res.results[core]["y"], np.float32)
        for j, g in enumerate(_own_blocks(par)):
            out[b, g * 128:(g + 1) * 128] = yc[j * 128:(j + 1) * 128]
    return out
```
